# Optimizing a Trainium2 kernel written in Bass

```python
import jax, jax.numpy as jnp
from jax import lax
import numpy as np

D_MODEL = 1024
BATCH = 8
SEQ = 4096
DEPTH = 2

CHUNK = 64
Q_BLOCK = 128
N_A_LAYERS = DEPTH // 2
N_B_LAYERS = DEPTH - N_A_LAYERS
CONV_WIDTH = 31
D_FF = 4 * D_MODEL
N_HEADS = 8
QK_NOPE_DIM = 128
QK_ROPE_DIM = 64
QK_HEAD_DIM = QK_NOPE_DIM + QK_ROPE_DIM
V_HEAD_DIM = 128
Q_LORA_RANK = 384
KV_LORA_RANK = 256
ROPE_BASE = 10000.0
RMS_EPS = 1e-6
LN_EPS = 1e-5

kernel_name = "yoco_conformer_conv_mla_hybrid"


def rms_norm(x, g):
    xf = x.astype(jnp.float32)
    y = xf * lax.rsqrt(jnp.mean(xf * xf, axis=-1, keepdims=True) + RMS_EPS)
    return (y * g.astype(jnp.float32)).astype(x.dtype)


def layer_norm(x, g, b):
    xf = x.astype(jnp.float32)
    mu = jnp.mean(xf, axis=-1, keepdims=True)
    xc = xf - mu
    var = jnp.mean(xc * xc, axis=-1, keepdims=True)
    y = xc * lax.rsqrt(var + LN_EPS) * g.astype(jnp.float32) + b.astype(jnp.float32)
    return y.astype(x.dtype)


def rope_tables(seq):
    inv = 1.0 / (ROPE_BASE ** (jnp.arange(0, QK_ROPE_DIM, 2, dtype=jnp.float32) / QK_ROPE_DIM))
    pos = jnp.arange(seq, dtype=jnp.float32)
    ang = pos[:, None] * inv[None, :]
    return jnp.cos(ang), jnp.sin(ang)


def apply_rope(x, cos, sin):
    xf = x.astype(jnp.float32)
    half = QK_ROPE_DIM // 2
    x1, x2 = xf[..., :half], xf[..., half:]
    out = jnp.concatenate([x1 * cos - x2 * sin, x1 * sin + x2 * cos], axis=-1)
    return out.astype(x.dtype)


def conformer_conv(h, w_pw1, b_pw1, w_dw, b_dw, ln_g, ln_b, w_pw2, b_pw2):
    u = h @ w_pw1 + b_pw1
    a, gate = jnp.split(u, 2, axis=-1)
    u = a * jax.nn.sigmoid(gate)
    u = jnp.pad(u, ((0, 0), (CONV_WIDTH - 1, 0), (0, 0)))
    u = lax.conv_general_dilated(u, w_dw[:, None, :], window_strides=(1,), padding='VALID',
                                 dimension_numbers=('NWC', 'WIO', 'NWC'),
                                 feature_group_count=D_MODEL) + b_dw
    u = jax.nn.silu(layer_norm(u, ln_g, ln_b))
    return u @ w_pw2 + b_pw2


def squared_relu_mlp(h, w1, w2):
    return jnp.square(jax.nn.relu(h @ w1)) @ w2


def mla_shared_kv(h, kv_in_g, w_dkv, kv_norm_g, w_kr, w_uk, w_uv, cos, sin):
    b, s, _ = h.shape
    hn = rms_norm(h, kv_in_g)
    c_kv = rms_norm(hn @ w_dkv, kv_norm_g)
    k_rope = apply_rope(hn @ w_kr, cos, sin)
    k_nope = (c_kv @ w_uk).reshape(b, s, N_HEADS, QK_NOPE_DIM)
    v = (c_kv @ w_uv).reshape(b, s, N_HEADS, V_HEAD_DIM)
    return k_nope, k_rope, v


def mla_attention(hn, w_dq, q_norm_g, w_uq, w_o, k_nope, k_rope, v, cos, sin):
    b, s, _ = hn.shape
    q = (rms_norm(hn @ w_dq, q_norm_g) @ w_uq).reshape(b, s, N_HEADS, QK_HEAD_DIM)
    q_nope = q[..., :QK_NOPE_DIM]
    q_rope = apply_rope(q[..., QK_NOPE_DIM:], cos[:, None, :], sin[:, None, :])
    scale = QK_HEAD_DIM ** -0.5
    key_chunk = jnp.arange(s) // CHUNK

    def block(i):
        start = i * Q_BLOCK
        qn = lax.dynamic_slice_in_dim(q_nope, start, Q_BLOCK, axis=1)
        qr = lax.dynamic_slice_in_dim(q_rope, start, Q_BLOCK, axis=1)
        sc = (jnp.einsum('bqhd,bkhd->bhqk', qn, k_nope, preferred_element_type=jnp.float32)
              + jnp.einsum('bqhr,bkr->bhqk', qr, k_rope, preferred_element_type=jnp.float32)) * scale
        q_chunk = (start + jnp.arange(Q_BLOCK)) // CHUNK
        mask = key_chunk[None, :] <= q_chunk[:, None]
        sc = jnp.where(mask[None, None], sc, -jnp.inf)
        p = jax.nn.softmax(sc, axis=-1).astype(v.dtype)
        return jnp.einsum('bhqk,bkhd->bqhd', p, v)

    o = lax.map(block, jnp.arange(s // Q_BLOCK))
    o = jnp.transpose(o, (1, 0, 2, 3, 4)).reshape(b, s, N_HEADS * V_HEAD_DIM)
    return o @ w_o


def setup_inputs(seed: int = 0) -> dict:
    key = jax.random.key(seed)
    ks = jax.random.split(key, 40)
    f32 = jnp.float32

    def nrm(k, shape, fan_in):
        return jax.random.normal(k, shape, f32) * (fan_in ** -0.5)

    def gain(k, shape):
        return 1.0 + 0.1 * jax.random.normal(k, shape, f32)

    def bias(k, shape):
        return 0.02 * jax.random.normal(k, shape, f32)

    D = D_MODEL
    return {
        "x": jax.random.normal(ks[0], (BATCH, SEQ, D), f32),
        "mix_pre_g": gain(ks[1], (DEPTH, D)),
        "mix_post_g": gain(ks[2], (DEPTH, D)),
        "ffn_pre_g": gain(ks[3], (DEPTH, D)),
        "ffn_post_g": gain(ks[4], (DEPTH, D)),
        "w_ff1": nrm(ks[5], (DEPTH, D, D_FF), D),
        "w_ff2": nrm(ks[6], (DEPTH, D_FF, D), D_FF),
        "conv_w_pw1": nrm(ks[7], (N_A_LAYERS, D, 2 * D), D),
        "conv_b_pw1": bias(ks[8], (N_A_LAYERS, 2 * D)),
        "conv_w_dw": nrm(ks[9], (N_A_LAYERS, CONV_WIDTH, D), CONV_WIDTH),
        "conv_b_dw": bias(ks[10], (N_A_LAYERS, D)),
        "conv_ln_g": gain(ks[11], (N_A_LAYERS, D)),
        "conv_ln_b": bias(ks[12], (N_A_LAYERS, D)),
        "conv_w_pw2": nrm(ks[13], (N_A_LAYERS, D, D), D),
        "conv_b_pw2": bias(ks[14], (N_A_LAYERS, D)),
        "mla_w_dq": nrm(ks[15], (N_B_LAYERS, D, Q_LORA_RANK), D),
        "mla_q_norm_g": gain(ks[16], (N_B_LAYERS, Q_LORA_RANK)),
        "mla_w_uq": nrm(ks[17], (N_B_LAYERS, Q_LORA_RANK, N_HEADS * QK_HEAD_DIM), Q_LORA_RANK),
        "mla_w_o": nrm(ks[18], (N_B_LAYERS, N_HEADS * V_HEAD_DIM, D), N_HEADS * V_HEAD_DIM),
        "kv_in_g": gain(ks[19], (D,)),
        "kv_w_dkv": nrm(ks[20], (D, KV_LORA_RANK), D),
        "kv_norm_g": gain(ks[21], (KV_LORA_RANK,)),
        "kv_w_kr": nrm(ks[22], (D, QK_ROPE_DIM), D),
        "kv_w_uk": nrm(ks[23], (KV_LORA_RANK, N_HEADS * QK_NOPE_DIM), KV_LORA_RANK),
        "kv_w_uv": nrm(ks[24], (KV_LORA_RANK, N_HEADS * V_HEAD_DIM), KV_LORA_RANK),
    }


def reference(x, mix_pre_g, mix_post_g, ffn_pre_g, ffn_post_g, w_ff1, w_ff2,
              conv_w_pw1, conv_b_pw1, conv_w_dw, conv_b_dw, conv_ln_g, conv_ln_b,
              conv_w_pw2, conv_b_pw2, mla_w_dq, mla_q_norm_g, mla_w_uq, mla_w_o,
              kv_in_g, kv_w_dkv, kv_norm_g, kv_w_kr, kv_w_uk, kv_w_uv):
    cos, sin = rope_tables(x.shape[1])
    h = x
    k_nope = k_rope = v = None
    for layer in range(DEPTH):
        if layer == N_A_LAYERS:
            k_nope, k_rope, v = mla_shared_kv(h, kv_in_g, kv_w_dkv, kv_norm_g, kv_w_kr,
                                              kv_w_uk, kv_w_uv, cos, sin)
        hn = rms_norm(h, mix_pre_g[layer])
        if layer < N_A_LAYERS:
            a = layer
            m = conformer_conv(hn, conv_w_pw1[a], conv_b_pw1[a], conv_w_dw[a], conv_b_dw[a],
                               conv_ln_g[a], conv_ln_b[a], conv_w_pw2[a], conv_b_pw2[a])
        else:
            bl = layer - N_A_LAYERS
            m = mla_attention(hn, mla_w_dq[bl], mla_q_norm_g[bl], mla_w_uq[bl], mla_w_o[bl],
                              k_nope, k_rope, v, cos, sin)
        h = h + rms_norm(m, mix_post_g[layer])
        f = squared_relu_mlp(rms_norm(h, ffn_pre_g[layer]), w_ff1[layer], w_ff2[layer])
        h = h + rms_norm(f, ffn_post_g[layer])
    return h
```

```python
import math
from contextlib import ExitStack

import numpy as np
import concourse.bass as bass
import concourse.mybir as mybir
from concourse.bass_utils import run_bass_kernel_spmd

F32 = mybir.dt.float32
BF16 = mybir.dt.bfloat16
ALU = mybir.AluOpType
AF = mybir.ActivationFunctionType

D = 1024
S = 4096
T = 512
NT = S // T
DFF = 4096
H = 8
CW = 31
HALO = CW - 1
QL = 384
KVL = 256
RMS_EPS = 1e-6
LN_EPS = 1e-5
SCALE = 192 ** -0.5
NCORES = 8
WBLK = 4096


def _proj_block(W, cols, kc):
    sub = W[:, cols].reshape(kc, 128, len(cols))
    return np.ascontiguousarray(sub.transpose(1, 0, 2)).reshape(128, kc * len(cols))


def _pad_block(b):
    out = np.zeros((128, WBLK), np.float32)
    out[:, : b.shape[1]] = b
    return out


def build_weight_blocks(inp):
    blocks = []
    names = {}

    def add(name, b):
        names[name] = len(blocks)
        blocks.append(_pad_block(b))

    ar = np.arange
    w1 = inp["conv_w_pw1"][0]
    for b in range(4):
        cols = np.concatenate([
            ar(128) + (2 * b) * 128, ar(128) + 1024 + (2 * b) * 128,
            ar(128) + (2 * b + 1) * 128, ar(128) + 1024 + (2 * b + 1) * 128])
        add(f"pw1_{b}", _proj_block(w1, cols, 8))
    w2 = inp["conv_w_pw2"][0]
    for b in range(2):
        add(f"pw2_{b}", _proj_block(w2, ar(512) + 512 * b, 8))
    def add_ffn(l):
        f1 = inp["w_ff1"][l]
        f2 = inp["w_ff2"][l]
        for b in range(8):
            add(f"ff1_{l}_{b}", _proj_block(f1, ar(512) + 512 * b, 8))
        for d in range(8):
            add(f"ff2_{l}_{d}", _proj_block(f2, ar(128) + 128 * d, 32))

    add_ffn(0)
    kr = inp["kv_w_kr"]
    sw = (ar(64) + 32) % 64
    kva = np.concatenate([inp["kv_w_dkv"], kr, kr, kr[:, sw], kr[:, sw]], axis=1)
    add("kva", _proj_block(kva, ar(512), 8))
    add("dq", _proj_block(inp["mla_w_dq"][0], ar(384), 8))
    kvb = np.concatenate([inp["kv_w_uk"], inp["kv_w_uv"]], axis=1)
    add("kvb", _proj_block(kvb, ar(2048), 2))
    uq = inp["mla_w_uq"][0]
    cols = []
    for h in range(8):
        cols.append(h * 192 + ar(128))
    for p in range(4):
        cols.append(p * 192 + 128 + ar(64))
        cols.append((p + 4) * 192 + 128 + ar(64))
    for p in range(4):
        cols.append(p * 192 + 128 + sw)
        cols.append((p + 4) * 192 + 128 + sw)
    cols = np.concatenate(cols)
    add("uq_0", _proj_block(uq, cols[:1024], 3))
    add("uq_1", _proj_block(uq, cols[1024:], 3))
    wo = inp["mla_w_o"][0]
    for b in range(2):
        add(f"wo_{b}", _proj_block(wo, ar(512) + 512 * b, 8))
    add_ffn(1)
    return np.stack(blocks, 0), names


VEC_SPECS = [("mix_pre_g0", 8), ("mix_pre_g1", 8), ("mix_post_g0", 8), ("mix_post_g1", 8),
             ("ffn_pre_g0", 8), ("ffn_pre_g1", 8), ("ffn_post_g0", 8), ("ffn_post_g1", 8),
             ("b_pw1", 16), ("w_dw", 8 * CW), ("b_dw", 8), ("ln_g", 8), ("ln_b", 8),
             ("b_pw2", 8), ("q_norm_g", 3), ("kv_in_g", 8), ("kv_norm_g", 2)]
VEC_OFF = {}
_o = 0
for _n, _c in VEC_SPECS:
    VEC_OFF[_n] = _o
    _o += _c
NVEC = _o


def build_vecs(inp):
    def colz(v):
        return np.ascontiguousarray(v.reshape(-1, 128).T)
    parts = {}
    for nm in ("mix_pre_g", "mix_post_g", "ffn_pre_g", "ffn_post_g"):
        for l in range(2):
            parts[f"{nm}{l}"] = colz(inp[nm][l])
    parts["b_pw1"] = colz(inp["conv_b_pw1"][0])
    wd = inp["conv_w_dw"][0]
    parts["w_dw"] = np.ascontiguousarray(
        wd.reshape(CW, 8, 128).transpose(2, 1, 0)).reshape(128, 8 * CW)
    parts["b_dw"] = colz(inp["conv_b_dw"][0])
    parts["ln_g"] = colz(inp["conv_ln_g"][0])
    parts["ln_b"] = colz(inp["conv_ln_b"][0])
    parts["b_pw2"] = colz(inp["conv_b_pw2"][0])
    parts["q_norm_g"] = colz(inp["mla_q_norm_g"][0])
    parts["kv_in_g"] = colz(inp["kv_in_g"])
    parts["kv_norm_g"] = colz(inp["kv_norm_g"])
    return np.ascontiguousarray(
        np.concatenate([parts[n] for n, _ in VEC_SPECS], axis=1).astype(np.float32))


def rope_tables():
    inv = (1.0 / (np.float32(10000.0) ** (np.arange(0, 64, 2, dtype=np.float32) / np.float32(64)))).astype(np.float32)
    pos = np.arange(S, dtype=np.float32)
    ang = (pos[:, None] * inv[None, :]).astype(np.float32)
    cos = np.cos(ang).astype(np.float32).T
    sin = np.sin(ang).astype(np.float32).T
    cos2 = np.concatenate([cos, cos, cos, cos], 0)
    sinS = np.concatenate([-sin, sin, -sin, sin], 0)
    return np.ascontiguousarray(cos2), np.ascontiguousarray(sinS)


class Res:
    __slots__ = ("name", "w", "r", "excl")

    def __init__(self, name, excl=False):
        self.name = name
        self.w = None
        self.r = {}
        self.excl = excl


class DSem:
    __slots__ = ("key", "sem", "val")

    def __init__(self, key, sem):
        self.key = key
        self.sem = sem
        self.val = 0


class _Eng:
    def __init__(self, name, h, sem):
        self.name = name
        self.h = h
        self.sem = sem
        self.count = 0
        self.seen = {}


class TK:
    def __init__(self):
        self.engs = {}
        self.nwaits = 0
        self.ninst = 0

    def add_engine(self, name, h, sem):
        self.engs[name] = _Eng(name, h, sem)

    def emit(self, eng, fn, reads=(), writes=(), signal=True, dsem=None):
        e = self.engs[eng]
        deps = {}

        def add(tok, same_ok):
            if tok is None:
                return
            key, sem, val, src = tok
            if src == eng and not same_ok:
                return
            cur = deps.get(key)
            if cur is None or cur[1] < val:
                deps[key] = (sem, val, src)

        for r in reads:
            add(r.w, True)
            if r.excl:
                for t in r.r.values():
                    add(t, False)
        for w in writes:
            add(w.w, False)
            for t in w.r.values():
                add(t, False)
        for key, (sem, val, src) in deps.items():
            if e.seen.get(key, 0) >= val:
                continue
            if src is not None:
                assert self.engs[src].count >= val, (eng, src, val, self.engs[src].count)
            e.h.wait_ge(sem, val)
            e.seen[key] = val
            self.nwaits += 1
        ins = fn()
        self.ninst += 1
        if dsem is not None:
            dsem.val += 16
            ins.then_inc(dsem.sem, 16)
            tok = (dsem.key, dsem.sem, dsem.val, None)
        elif signal:
            e.count += 1
            ins.then_inc(e.sem, 1)
            tok = (eng, e.sem, e.count, eng)
        else:
            tok = (eng, e.sem, e.count + 1, eng)
        for w in writes:
            w.w = tok
            w.r = {}
        for r in reads:
            cur = r.r.get(tok[0])
            if cur is None or cur[2] < tok[2]:
                r.r[tok[0]] = tok
        return tok

    def wait_all(self, eng, toks):
        e = self.engs[eng]
        for tok in toks:
            if tok is None:
                continue
            key, sem, val, src = tok
            if e.seen.get(key, 0) >= val:
                continue
            e.h.wait_ge(sem, val)
            e.seen[key] = val


class _Stop(Exception):
    pass


def build_program(nblk, wnames, ntiles=NT, dumps=(), skip_cast=False, stop_after=None):
    nc = bass.Bass("TRN2", target_bir_lowering=False)
    xT = nc.dram_tensor("xT", [D, S], F32, kind="ExternalInput").ap()
    w32 = nc.dram_tensor("w32", [nblk, 128, WBLK], F32, kind="ExternalInput").ap()
    vecs_d = nc.dram_tensor("vecs", [128, NVEC], F32, kind="ExternalInput").ap()
    cos_d = nc.dram_tensor("cos2", [128, S], F32, kind="ExternalInput").ap()
    sin_d = nc.dram_tensor("sinS", [128, S], F32, kind="ExternalInput").ap()
    outT = nc.dram_tensor("outT", [D, S], F32, kind="ExternalOutput").ap()
    wbf = nc.dram_tensor("wbf", [nblk, 128, WBLK], BF16, kind="Internal").ap()
    Kc = nc.dram_tensor("Kc", [H, 128, S], BF16, kind="Internal").ap()
    diagbf = nc.dram_tensor("diagbf", [8, 128, WBLK], BF16, kind="Internal").ap()
    Vc = nc.dram_tensor("Vc", [H, 128, S], BF16, kind="Internal").ap()
    dump_aps = {}
    for nm, shape, dt in dumps:
        dump_aps[nm] = nc.dram_tensor("dbg_" + nm, list(shape), dt, kind="ExternalOutput").ap()

    es = ExitStack()
    with es:
        def sb(name, shape, dt):
            return es.enter_context(nc.sbuf_tensor(name, list(shape), dt))

        def sem(name):
            return es.enter_context(nc.semaphore(name))

        tk = TK()
        tk.add_engine("pe", nc.tensor, sem("s_pe"))
        tk.add_engine("act", nc.scalar, sem("s_act"))
        tk.add_engine("dve", nc.vector, sem("s_dve"))
        tk.add_engine("pool", nc.gpsimd, sem("s_pool"))
        tk.add_engine("sp", nc.sync, sem("s_sp"))

        h_a = sb("h", [128, 8, T], F32)
        h_b = sb("h2", [128, 8, T], F32)
        h_t = h_a
        tA = sb("tA", [128, 8, T], F32)
        hn = sb("hn", [128, 8, T], BF16)
        sq = sb("sq", [128, 8, T], BF16)
        mid = sb("mid", [128, 32, T], BF16)
        ubuf = sb("ubuf", [128, 8, HALO + T], BF16)
        sig = sb("sig", [128, 2, T], F32)
        wring = sb("wring", [128, 3, WBLK], BF16)
        rstd = sb("rstd", [128, T], F32)
        mean = sb("mean", [128, T], F32)
        t1 = sb("t1", [128, T], F32)
        t2 = sb("t2", [128, T], F32)
        rden = sb("rden", [128, T], F32)
        rstd2 = sb("rstd2", [128, T], F32)
        accb = sb("accb", [128, 2, T], BF16)
        rscr = sb("rscr", [128, T], F32)
        rscr2 = sb("rscr2", [128, T], F32)
        ckv = sb("ckv", [128, 2, T], BF16)
        cq = sb("cq", [128, 3, T], BF16)
        krope = sb("krope", [128, S], BF16)
        cos_t = sb("cos_t", [128, T], F32)
        sin_t = sb("sin_t", [128, T], F32)
        kring = sb("kring", [128, 2, S], BF16)
        vring = sb("vring", [128, 2, S], BF16)
        vecs = sb("vecs_sb", [128, NVEC], F32)
        ones = sb("ones", [128, 128], BF16)
        ident = sb("ident", [128, 128], BF16)
        epsc = sb("epsc", [128, 2], F32)
        identf = sb("identf", [128, 128], F32)

        ps = [es.enter_context(nc.psum_tensor(f"ps{i}", [128, T], F32)) for i in range(8)]

        R_ha = [Res(f"h{c}") for c in range(8)]
        R_hb = [Res(f"hb{c}") for c in range(8)]
        R_h = R_ha
        hbufs = [(h_a, R_ha), (h_b, R_hb)]
        R_tA = [Res(f"tA{c}") for c in range(8)]
        R_hn = [Res(f"hn{c}") for c in range(8)]
        R_sq = [Res(f"sq{c}") for c in range(8)]
        R_mid = [Res(f"mid{c}") for c in range(32)]
        R_ub = [Res(f"ub{c}") for c in range(8)]
        R_sig = [Res("sig0"), Res("sig1")]
        R_w = [Res(f"w{i}") for i in range(3)]
        R_rstd, R_mean, R_t1, R_t2, R_rden = Res("rstd"), Res("mean"), Res("t1"), Res("t2"), Res("rden")
        R_rscr, R_rscr2 = Res("rscr"), Res("rscr2")
        R_rstd2 = Res("rstd2")
        R_accb = [Res("accb0"), Res("accb1")]

        R_ckv = [Res("ckv0"), Res("ckv1")]
        R_cq = [Res(f"cq{i}") for i in range(3)]
        R_krope = Res("krope")
        R_cos, R_sin = Res("cos"), Res("sin")
        R_kr = [Res("kr0"), Res("kr1")]
        R_vr = [Res("vr0"), Res("vr1")]
        R_const = Res("const")
        R_ps = [Res(f"ps{i}", excl=True) for i in range(8)]
        R_wbf = [Res(f"wbf{i}") for i in range(nblk)]
        R_Kcs = [Res(f"Kc{i}") for i in range(NT)]
        R_Vcs = [Res(f"Vc{i}") for i in range(NT)]
        R_outs = [Res("out0"), Res("out1")]

        ds_w = [DSem(f"dw{i}", sem(f"d_w{i}")) for i in range(3)]
        ds_x = [DSem("dx0", sem("d_x0")), DSem("dx1", sem("d_x1"))]
        ds_o = [DSem("do0", sem("d_o0")), DSem("do1", sem("d_o1"))]
        ds_c = DSem("dc", sem("d_c"))
        ds_cos = DSem("dcos", sem("d_cos"))
        ds_sin = DSem("dsin", sem("d_sin"))
        ds_ks = DSem("dks", sem("d_ks"))
        ds_vs = DSem("dvs", sem("d_vs"))
        ds_kl = [DSem(f"dkl{i}", sem(f"d_kl{i}")) for i in range(2)]
        ds_vl = [DSem(f"dvl{i}", sem(f"d_vl{i}")) for i in range(2)]
        ds_pro = [DSem(f"dpro{i}", sem(f"d_pro{i}")) for i in range(4)]
        ds_dbg = {nm: DSem("ddbg_" + nm, sem("d_dbg_" + nm)) for nm, _, _ in dumps}

        def V(name, i=0):
            o = VEC_OFF[name] + i
            return vecs[:, o:o + 1]

        marks = []
        mmc = [0]

        def mark(label):
            marks.append((mmc[0], label))

        def mm(out, r_out, lhsT, r_l, rhs, r_r, start, stop, signal=None):
            mmc[0] += 1
            if signal is None:
                signal = stop
            rd = [r_l, r_r] if isinstance(r_r, Res) else [r_l] + list(r_r)
            return tk.emit("pe", lambda: nc.tensor.matmul(out, lhsT, rhs, start=start, stop=stop),
                           reads=rd, writes=[r_out], signal=signal)

        def act(out, r_out, in_, r_in, func, bias=None, scale=None, extra_reads=()):
            kw = {}
            if bias is not None:
                kw["bias"] = bias
            if scale is not None:
                kw["scale"] = scale
            rins = r_in if isinstance(r_in, (list, tuple)) else [r_in]
            return tk.emit("act", lambda: nc.scalar.activation(out, in_, func, **kw),
                           reads=list(rins) + [R_const] + list(extra_reads), writes=[r_out])

        def ts(eng, out, r_out, in0, r_in, s1, s2, op0, op1=None):
            h = nc.vector if eng == "dve" else nc.gpsimd
            if op1 is None:
                f = lambda: h.tensor_scalar(out, in0, s1, None, op0)
            else:
                f = lambda: h.tensor_scalar(out, in0, s1, s2, op0, op1)
            return tk.emit(eng, f, reads=[r_in, R_const], writes=[r_out])

        def tt(eng, out, r_out, in0, r0, in1, r1, op):
            h = nc.vector if eng == "dve" else nc.gpsimd
            return tk.emit(eng, lambda: h.tensor_tensor(out, in0, in1, op),
                           reads=[r0, r1], writes=[r_out])

        def stt(out, r_out, in0, r0, scalar, in1, r1, op0, op1):
            return tk.emit("dve", lambda: nc.vector.scalar_tensor_tensor(out, in0, scalar, in1, op0, op1),
                           reads=[r0, r1, R_const], writes=[r_out])

        def dma(eng, out, r_out, in_, r_in, dsem, **kw):
            h = nc.sync if eng == "sp" else nc.gpsimd
            rd = [r_in] if isinstance(r_in, Res) else list(r_in)
            wr = [r_out] if isinstance(r_out, Res) else list(r_out)
            return tk.emit(eng, lambda: h.dma_start(out=out, in_=in_, **kw), reads=rd, writes=wr, dsem=dsem)

        def eps_ap(eps):
            return epsc[:, 0:1] if eps == RMS_EPS else epsc[:, 1:2]

        tk.emit("pool", lambda: nc.gpsimd.memset(epsc[:, 0:1], RMS_EPS), writes=[R_const])
        tk.emit("pool", lambda: nc.gpsimd.memset(epsc[:, 1:2], LN_EPS), writes=[R_const])
        dma("sp", vecs[:, :], R_const, vecs_d[:, :], Res("vecs_d"), ds_c)
        tk.emit("pool", lambda: nc.gpsimd.memset(ones[:, :], 1.0), writes=[R_const])
        tk.emit("pool", lambda: nc.gpsimd.memset(identf[:, :], 0.0), writes=[R_const])
        tk.emit("pool", lambda: nc.gpsimd.affine_select(
            identf[:, :], identf[:, :], [[-1, 128]], ALU.not_equal, 1.0, base=0, channel_multiplier=1),
            reads=[R_const], writes=[R_const])
        tk.emit("pool", lambda: nc.gpsimd.tensor_copy(ident[:, :], identf[:, :]),
                reads=[R_const], writes=[R_const])
        tk.emit("pool", lambda: nc.gpsimd.memset(ubuf[:, :, :], 0.0), writes=R_ub)

        R_diag = [Res(f"diag{c}") for c in range(8)]
        ds_diag = [DSem(f"ddg{c}", sem(f"d_dg{c}")) for c in range(8)]

        def gen_diags():
            for c in range(8):
                q4 = c % 4
                stage = mid[:, 8 * q4:8 * q4 + 8, :].rearrange("p a t -> p (a t)")
                dst = stage[:, 0:CW * 128].rearrange("p (k m) -> p k m", k=CW)
                o = VEC_OFF["w_dw"] + c * CW
                rr = R_mid[8 * q4:8 * q4 + 8]
                tk.emit("pool", lambda dst=dst, o=o: nc.gpsimd.tensor_tensor(
                    dst, ident[:, :].unsqueeze(1).broadcast_to([128, CW, 128]),
                    vecs[:, o:o + CW].unsqueeze(2).broadcast_to([128, CW, 128]), ALU.mult),
                    reads=[R_const], writes=rr)
                tk.emit("pool", lambda stage=stage: nc.gpsimd.memset(stage[:, CW * 128:WBLK], 0.0), writes=rr)
                dma("pool", diagbf[c, :, :], R_diag[c], stage[:, :], rr, ds_diag[c])

        if not skip_cast:
            lanes = [Res(f"pro_lane{i}") for i in range(4)]

            def cast_blk(b):
                tk.emit("pool", lambda b=b: nc.gpsimd.dma_start(
                    out=wbf[b], in_=w32[b], max_dma_last_dim=2048 * 4),
                    reads=[], writes=[R_wbf[b], lanes[b % 4]], dsem=ds_pro[b % 4])

            for b in range(min(4, nblk)):
                cast_blk(b)
            gen_diags()
            for b in range(4, nblk):
                cast_blk(b)
        else:
            gen_diags()

        sched = []
        for j in range(ntiles):
            sched += [("w", "pw1_0"), ("w", "pw1_1"), ("diag", 0), ("diag", 1), ("w", "pw1_2"),
                      ("diag", 2), ("diag", 3), ("w", "pw1_3"), ("diag", 4), ("diag", 5),
                      ("diag", 6), ("diag", 7), ("w", "pw2_0"), ("w", "pw2_1")]
            sched += [("w", f"ff1_0_{b}") for b in range(8)]
            sched += [("w", f"ff2_0_{b}") for b in range(8)]
            sched += [("w", "kva"), ("w", "dq"), ("w", "kvb"), ("w", "uq_0"), ("w", "uq_1"),
                      ("w", "wo_0"), ("w", "wo_1")]
            sched += [("w", f"ff1_1_{b}") for b in range(8)]
            sched += [("w", f"ff2_1_{b}") for b in range(8)]
        st = {"issued": 0, "next": 0}

        def issue_one():
            i = st["issued"]
            if i >= len(sched):
                return
            slot = i % 3
            kind, arg = sched[i]
            if kind == "w":
                b = wnames[arg]
                dma("sp", wring[:, slot, :], R_w[slot], wbf[b], R_wbf[b], ds_w[slot])
            else:
                c = arg
                dma("sp", wring[:, slot, :], R_w[slot], diagbf[c], R_diag[c], ds_w[slot])
            st["issued"] += 1

        def wnext(expect):
            i = st["next"]
            assert sched[i] == expect, (sched[i], expect)
            while st["issued"] < min(len(sched), i + 3):
                issue_one()
            st["next"] += 1
            return i % 3

        pscur = {"mm": 0}

        def next_mm_bank():
            b = pscur["mm"]
            pscur["mm"] = (b + 1) % 4
            return b

        def stats(srcs, bank):
            n = len(srcs)
            for i, (ap, r) in enumerate(srcs):
                mm(ps[bank][:, :], R_ps[bank], ones[:, :], R_const, ap, r, i == 0, i == n - 1)

        def rstd_from(bank, dim, eps, out=None, r_out=None, scr=None, r_scr=None):
            out = rstd if out is None else out
            r_out = R_rstd if r_out is None else r_out
            scr = rscr if scr is None else scr
            r_scr = R_rscr if r_scr is None else r_scr
            act(scr[:, :], r_scr, ps[bank][:, :], R_ps[bank], AF.Ln, bias=eps_ap(eps), scale=1.0 / dim)
            act(out[:, :], r_out, scr[:, :], r_scr, AF.Exp, scale=-0.5)
            return

        def _rstd_from_old(bank, dim, eps):
            act(rscr[:, :], R_rscr, ps[bank][:, :], R_ps[bank], AF.Ln, bias=eps_ap(eps), scale=1.0 / dim)
            act(rstd[:, :], R_rstd, rscr[:, :], R_rscr, AF.Exp, scale=-0.5)

        def rms_stats(src, r_src, nch, dim):
            dump("ck_load", None, None)
            for c in range(nch):
                act(sq[:, c, :], R_sq[c], src[:, c, :], r_src[c], AF.Square)
            dump("ck_sq", None, None)
            stats([(sq[:, c, :], R_sq[c]) for c in range(nch)], 4)
            dump("ck_stats", None, None)
            rstd_from(4, dim, RMS_EPS)
            dump("ck_rstd", None, None)

        def prenorm(gname, src=None, r_src=None, dst=None, r_dst=None, nch=8, rs=None, r_rs=None):
            src = h_t if src is None else src
            r_src = R_h if r_src is None else r_src
            dst = hn if dst is None else dst
            r_dst = R_hn if r_dst is None else r_dst
            rs = rstd if rs is None else rs
            r_rs = R_rstd if r_rs is None else r_rs
            for c in range(nch):
                stt(dst[:, c, :], r_dst[c], src[:, c, :], r_src[c], V(gname, c), rs[:, :], r_rs,
                    ALU.mult, ALU.mult)

        def early_mixpre(jn):
            hb, Rb = hbufs[jn % 2]
            for c in range(8):
                act(hn[:, c, :], R_hn[c], hb[:, c, :], Rb[c], AF.Square)
            stats([(hn[:, c, :], R_hn[c]) for c in range(8)], 5)
            rstd_from(5, D, RMS_EPS, out=rstd2, r_out=R_rstd2, scr=rscr2, r_scr=R_rscr2)
            prenorm("mix_pre_g0", src=hb, r_src=Rb, rs=rstd2, r_rs=R_rstd2)

        stores = []
        pending = []

        def run_pending(n=None):
            k = len(pending) if n is None else min(n, len(pending))
            for _ in range(k):
                pending.pop(0)()

        def postnorm(gname, defer=False, after=None):
            stats([(sq[:, c, :], R_sq[c]) for c in range(8)], 4)
            rstd_from(4, D, RMS_EPS)
            hb, Rb = h_t, R_h

            def apply(c):
                stt(tA[:, c, :], R_tA[c], tA[:, c, :], R_tA[c], V(gname, c), rstd[:, :], R_rstd,
                    ALU.mult, ALU.mult)
                tt("dve", hb[:, c, :], Rb[c], hb[:, c, :], Rb[c], tA[:, c, :], R_tA[c], ALU.add)
                if after is not None:
                    after(c)

            for c in range(8):
                if defer:
                    pending.append(lambda c=c: apply(c))
                else:
                    apply(c)

        def proj8(blockname, nout, evac, kouter=False, mid_hook=None):
            slot = wnext(("w", blockname))
            wv = wring[:, slot, :].rearrange("p (k m) -> p k m", k=8)
            if kouter:
                banks = [next_mm_bank() for _ in range(nout)]
                for k in range(8):
                    for cc in range(nout):
                        mm(ps[banks[cc]][:, :], R_ps[banks[cc]], wv[:, k, cc * 128:(cc + 1) * 128], R_w[slot],
                           hn[:, k, :], R_hn[k], k == 0, k == 7)
                if mid_hook is not None:
                    mid_hook()
                for cc in range(nout):
                    evac(cc, banks[cc])
                return
            for cc in range(nout):
                bank = next_mm_bank()
                for k in range(8):
                    mm(ps[bank][:, :], R_ps[bank], wv[:, k, cc * 128:(cc + 1) * 128], R_w[slot],
                       hn[:, k, :], R_hn[k], k == 0, k == 7)
                evac(cc, bank)

        def hn_scale(c, l):
            ts("dve", hn[:, c, :], R_hn[c], h_t[:, c, :], R_h[c], V(f"ffn_pre_g{l}", c), None, ALU.mult)

        def ffn(l, hook=None, pre_scaled=False):
            mark(f"ffn{l}:prenorm")
            if not pre_scaled:
                for c in range(8):
                    hn_scale(c, l)
            for c in range(8):
                act(sq[:, c, :], R_sq[c], h_t[:, c, :], R_h[c], AF.Square)

            def stats_hook():
                stats([(sq[:, c, :], R_sq[c]) for c in range(8)], 4)
                rstd_from(4, D, RMS_EPS)

            for b in range(8):
                def ev(cc, bank, b=b):
                    m = 4 * b + cc
                    stt(mid[:, m, :], R_mid[m], ps[bank][:, :], R_ps[bank], 0.0, rstd[:, :], R_rstd,
                        ALU.max, ALU.mult)
                    tt("dve", mid[:, m, :], R_mid[m], mid[:, m, :], R_mid[m], mid[:, m, :], R_mid[m], ALU.mult)
                proj8(f"ff1_{l}_{b}", 4, ev, kouter=(b == 0), mid_hook=stats_hook if b == 0 else None)
            mark(f"ffn{l}:ff2")
            for d in range(8):
                slot = wnext(("w", f"ff2_{l}_{d}"))
                wv = wring[:, slot, :].rearrange("p (k m) -> p k m", k=32)
                bank = next_mm_bank()
                for k in range(32):
                    mm(ps[bank][:, :], R_ps[bank], wv[:, k, :], R_w[slot], mid[:, k, :], R_mid[k],
                       k == 0, k == 31)
                ts("dve", tA[:, d, :], R_tA[d], ps[bank][:, :], R_ps[bank], 1.0, None, ALU.mult)
                act(sq[:, d, :], R_sq[d], ps[bank][:, :], R_ps[bank], AF.Square)
                if d == 1 and hook is not None:
                    hook()
            mark(f"ffn{l}:post")
            postnorm(f"ffn_post_g{l}", defer=(l == 1))

        def dump(nm, ap, rs):
            if nm in dump_aps:
                dma("sp", dump_aps[nm], Res("dbg_" + nm), ap, rs, ds_dbg[nm])
            if stop_after == nm:
                raise _Stop()

        def tile_body(j):
            nonlocal h_t, R_h
            c0 = j * T
            h_t, R_h = hbufs[j % 2]
            if j == 0:
                dma("sp", h_t[:, :, :], R_h, xT[:, 0:T].rearrange("(c p) t -> p c t", p=128),
                    Res("x_d"), ds_x[0])
            dma("sp", cos_t[:, :], R_cos, cos_d[:, c0:c0 + T], Res("cos_d"), ds_cos)
            dma("sp", sin_t[:, :], R_sin, sin_d[:, c0:c0 + T], Res("sin_d"), ds_sin)

            mark(f"t{j}:mixpre0")
            if j == 0:
                rms_stats(h_t, R_h, 8, D)
                prenorm("mix_pre_g0")
            if j == 0:
                dump("hn0", hn[:, :, :], R_hn)

            def glu_block(b):
                slot = wnext(("w", f"pw1_{b}"))
                wv = wring[:, slot, :].rearrange("p (k m) -> p k m", k=8)
                if b == 0:
                    gb = [next_mm_bank() for _ in range(4)]
                    for k in range(8):
                        for g4 in range(4):
                            mm(ps[gb[g4]][:, :], R_ps[gb[g4]], wv[:, k, g4 * 128:(g4 + 1) * 128], R_w[slot],
                               hn[:, k, :], R_hn[k], k == 0, k == 7)
                for i in range(2):
                    c = 2 * b + i
                    if b == 0:
                        ba, bg = gb[2 * i], gb[2 * i + 1]
                    else:
                        ba = next_mm_bank()
                        for k in range(8):
                            mm(ps[ba][:, :], R_ps[ba], wv[:, k, (2 * i) * 128:(2 * i + 1) * 128], R_w[slot],
                               hn[:, k, :], R_hn[k], k == 0, k == 7)
                        bg = next_mm_bank()
                        for k in range(8):
                            mm(ps[bg][:, :], R_ps[bg], wv[:, k, (2 * i + 1) * 128:(2 * i + 2) * 128], R_w[slot],
                               hn[:, k, :], R_hn[k], k == 0, k == 7)
                    si = c % 2
                    act(sig[:, si, :], R_sig[si], ps[bg][:, :], R_ps[bg], AF.Sigmoid, bias=V("b_pw1", 8 + c))
                    stt(ubuf[:, c, HALO:HALO + T], R_ub[c], ps[ba][:, :], R_ps[ba], V("b_pw1", c),
                        sig[:, si, :], R_sig[si], ALU.add, ALU.mult)
                    run_pending(1)

            def conv_chunk(c):
                slot = wnext(("diag", c))
                wv = wring[:, slot, 0:CW * 128].rearrange("p (k m) -> p k m", k=CW)
                bank = next_mm_bank()
                for k in range(CW):
                    mm(ps[bank][:, :], R_ps[bank], wv[:, k, :], R_w[slot], ubuf[:, c, k:k + T], R_ub[c],
                       k == 0, k == CW - 1)
                act(tA[:, c, :], R_tA[c], ps[bank][:, :], R_ps[bank], AF.Identity, bias=V("b_dw", c))
                act(mid[:, c, :], R_mid[c], ps[bank][:, :], R_ps[bank], AF.Identity, bias=V("b_dw", c))
                act(sq[:, c, :], R_sq[c], ps[bank][:, :], R_ps[bank], AF.Square, bias=V("b_dw", c))
                tk.emit("pool", lambda: nc.gpsimd.tensor_copy(ubuf[:, c, 0:HALO], ubuf[:, c, T:T + HALO]),
                        reads=[R_ub[c]], writes=[R_ub[c]])

            mark(f"t{j}:pw1conv")
            glu_block(0)
            glu_block(1)
            conv_chunk(0)
            conv_chunk(1)
            glu_block(2)
            conv_chunk(2)
            conv_chunk(3)
            glu_block(3)
            run_pending()
            while stores:
                stores.pop(0)()
            for c in range(4, 8):
                conv_chunk(c)
            if j == 0:
                dump("conv0", tA[:, :, :], R_tA)

            mark(f"t{j}:LN")
            stats([(mid[:, c, :], R_mid[c]) for c in range(8)], 4)
            stats([(sq[:, c, :], R_sq[c]) for c in range(8)], 5)
            ts("dve", mean[:, :], R_mean, ps[4][:, :], R_ps[4], 1.0 / D, None, ALU.mult)
            tt("dve", t1[:, :], R_t1, mean[:, :], R_mean, mean[:, :], R_mean, ALU.mult)
            stt(t2[:, :], R_t2, ps[5][:, :], R_ps[5], 1.0 / D, t1[:, :], R_t1, ALU.mult, ALU.subtract)
            act(rscr[:, :], R_rscr, t2[:, :], R_t2, AF.Ln, bias=eps_ap(LN_EPS), scale=1.0)
            act(rstd[:, :], R_rstd, rscr[:, :], R_rscr, AF.Exp, scale=-0.5)
            for c in range(8):
                tt("dve", tA[:, c, :], R_tA[c], tA[:, c, :], R_tA[c], mean[:, :], R_mean, ALU.subtract)
                tt("dve", tA[:, c, :], R_tA[c], tA[:, c, :], R_tA[c], rstd[:, :], R_rstd, ALU.mult)
                act(hn[:, c, :], R_hn[c], tA[:, c, :], R_tA[c], AF.Silu, bias=V("ln_b", c), scale=V("ln_g", c))
            if j == 0:
                dump("lnsilu0", hn[:, :, :], R_hn)

            for b in range(2):
                def ev(cc, bank, b=b):
                    d = 4 * b + cc
                    act(tA[:, d, :], R_tA[d], ps[bank][:, :], R_ps[bank], AF.Identity, bias=V("b_pw2", d))
                    act(sq[:, d, :], R_sq[d], ps[bank][:, :], R_ps[bank], AF.Square, bias=V("b_pw2", d))
                proj8(f"pw2_{b}", 4, ev, kouter=(b == 0))
            mark(f"t{j}:post_mix0")
            postnorm("mix_post_g0", after=lambda c: hn_scale(c, 0))
            if j == 0:
                dump("h_mix0", h_t[:, :, :], R_h)
            if j + 1 < ntiles:
                hn_, Rn_ = hbufs[(j + 1) % 2]
                dma("sp", hn_[:, :, :], Rn_, xT[:, c0 + T:c0 + 2 * T].rearrange("(c p) t -> p c t", p=128),
                    Res("x_d"), ds_x[(j + 1) % 2])
            mark(f"t{j}:ffn0")
            ffn(0, pre_scaled=True)
            if j == 0:
                dump("h_l0", h_t[:, :, :], R_h)

            mark(f"t{j}:kvnorm")
            if j > 0:
                for hh_ in (0, 1):
                    sl_ = hh_ % 2
                    dma("sp", kring[:, sl_, 0:c0], R_kr[sl_], Kc[hh_, :, 0:c0], R_Kcs[0:j], ds_kl[sl_])
                    dma("sp", vring[:, sl_, 0:c0], R_vr[sl_], Vc[hh_, :, 0:c0], R_Vcs[0:j], ds_vl[sl_])
            rms_stats(h_t, R_h, 8, D)
            prenorm("kv_in_g")
            prenorm("mix_pre_g1", dst=mid, r_dst=R_mid)
            slot = wnext(("w", "kva"))
            wv = wring[:, slot, :].rearrange("p (k m) -> p k m", k=8)
            kb = [next_mm_bank() for _ in range(4)]
            for k in range(8):
                for g4 in range(4):
                    mm(ps[kb[g4]][:, :], R_ps[kb[g4]], wv[:, k, g4 * 128:(g4 + 1) * 128], R_w[slot],
                       hn[:, k, :], R_hn[k], k == 0, k == 7)
            for cc in range(2):
                ts("dve", tA[:, cc, :], R_tA[cc], ps[kb[cc]][:, :], R_ps[kb[cc]], 1.0, None, ALU.mult)
                act(sq[:, cc, :], R_sq[cc], ps[kb[cc]][:, :], R_ps[kb[cc]], AF.Square)
            bka, bkb = kb[2], kb[3]
            tt("dve", t1[:, :], R_t1, ps[bka][:, :], R_ps[bka], cos_t[:, :], R_cos, ALU.mult)
            tt("dve", t2[:, :], R_t2, ps[bkb][:, :], R_ps[bkb], sin_t[:, :], R_sin, ALU.mult)
            tt("dve" if j == 0 else "pool", krope[:, c0:c0 + T], R_krope, t1[:, :], R_t1, t2[:, :], R_t2, ALU.add)
            mark(f"t{j}:dq")
            slot = wnext(("w", "dq"))
            wv = wring[:, slot, 0:8 * QL].rearrange("p (k m) -> p k m", k=8)
            qb = [next_mm_bank() for _ in range(3)]
            for k in range(8):
                for cc in range(3):
                    mm(ps[qb[cc]][:, :], R_ps[qb[cc]], wv[:, k, cc * 128:(cc + 1) * 128], R_w[slot],
                       mid[:, k, :], R_mid[k], k == 0, k == 7)
            for cc in range(3):
                ts("dve", tA[:, 2 + cc, :], R_tA[2 + cc], ps[qb[cc]][:, :], R_ps[qb[cc]], 1.0, None, ALU.mult)
                act(sq[:, 2 + cc, :], R_sq[2 + cc], ps[qb[cc]][:, :], R_ps[qb[cc]], AF.Square)
            stats([(sq[:, cc, :], R_sq[cc]) for cc in range(2)], 5)
            rstd_from(5, KVL, RMS_EPS, out=rstd2, r_out=R_rstd2, scr=rscr2, r_scr=R_rscr2)
            for cc in range(2):
                stt(ckv[:, cc, :], R_ckv[cc], tA[:, cc, :], R_tA[cc], V("kv_norm_g", cc), rstd2[:, :], R_rstd2,
                    ALU.mult, ALU.mult)
            stats([(sq[:, 2 + cc, :], R_sq[2 + cc]) for cc in range(3)], 4)
            rstd_from(4, QL, RMS_EPS)
            for cc in range(3):
                stt(cq[:, cc, :], R_cq[cc], tA[:, 2 + cc, :], R_tA[2 + cc], V("q_norm_g", cc), rstd[:, :], R_rstd,
                    ALU.mult, ALU.mult)
            if j == 0:
                dump("ckv", ckv[:, :, :], R_ckv)
            mark(f"t{j}:kvb")
            slot = wnext(("w", "kvb"))
            wv = wring[:, slot, :].rearrange("p (k m) -> p k m", k=2)
            for hh in range(H):
                bank = next_mm_bank()
                for k in range(2):
                    mm(ps[bank][:, :], R_ps[bank], wv[:, k, hh * 128:(hh + 1) * 128], R_w[slot],
                       ckv[:, k, :], R_ckv[k], k == 0, k == 1)
                if hh % 2 == 0:
                    ts("dve", mid[:, 16 + hh, :], R_mid[16 + hh], ps[bank][:, :], R_ps[bank], 1.0, None, ALU.mult)
                else:
                    act(mid[:, 16 + hh, :], R_mid[16 + hh], ps[bank][:, :], R_ps[bank], AF.Identity)
            dump("ck_kn", None, None)
            for tb in range(4):
                for half in range(2):
                    bank = next_mm_bank()
                    for k in range(2):
                        mm(ps[bank][:, :], R_ps[bank], ckv[:, k, tb * 128:(tb + 1) * 128], R_ckv[k],
                           wv[:, k, 1024 + half * 512:1024 + (half + 1) * 512], R_w[slot], k == 0, k == 1)
                    dst = mid[:, 24 + 4 * half:24 + 4 * half + 4, tb * 128:(tb + 1) * 128]
                    src = ps[bank][:, :].rearrange("p (h d) -> p h d", h=4)
                    rw = R_mid[24 + 4 * half:24 + 4 * half + 4]
                    if half == 0:
                        tk.emit("dve", lambda: nc.vector.tensor_copy(dst, src), reads=[R_ps[bank]], writes=rw)
                    else:
                        tk.emit("act", lambda: nc.scalar.copy(dst, src), reads=[R_ps[bank]], writes=rw)
            dump("ck_v", None, None)
            dma("sp", Kc[:, :, c0:c0 + T].rearrange("h p t -> p h t"), R_Kcs[j], mid[:, 16:24, :], R_mid[16:24], ds_ks)
            dma("sp", Vc[:, :, c0:c0 + T].rearrange("h p t -> p h t"), R_Vcs[j], mid[:, 24:32, :], R_mid[24:32], ds_vs)

            dump("ck_kvst", None, None)
            dump("ck_dq", None, None)
            mark(f"t{j}:uq")
            slot = wnext(("w", "uq_0"))
            wv = wring[:, slot, 0:3 * 1024].rearrange("p (k m) -> p k m", k=3)
            for hh in range(H):
                bank = next_mm_bank()
                for k in range(3):
                    mm(ps[bank][:, :], R_ps[bank], wv[:, k, hh * 128:(hh + 1) * 128], R_w[slot],
                       cq[:, k, :], R_cq[k], k == 0, k == 2)
                if hh % 2 == 0:
                    ts("dve", mid[:, hh, :], R_mid[hh], ps[bank][:, :], R_ps[bank], 1.0, None, ALU.mult)
                else:
                    act(mid[:, hh, :], R_mid[hh], ps[bank][:, :], R_ps[bank], AF.Identity)
            dump("ck_uq0", None, None)
            slot = wnext(("w", "uq_1"))
            wv = wring[:, slot, 0:3 * 1024].rearrange("p (k m) -> p k m", k=3)
            for p in range(4):
                ba = next_mm_bank()
                for k in range(3):
                    mm(ps[ba][:, :], R_ps[ba], wv[:, k, p * 128:(p + 1) * 128], R_w[slot],
                       cq[:, k, :], R_cq[k], k == 0, k == 2)
                bb = next_mm_bank()
                for k in range(3):
                    mm(ps[bb][:, :], R_ps[bb], wv[:, k, (4 + p) * 128:(5 + p) * 128], R_w[slot],
                       cq[:, k, :], R_cq[k], k == 0, k == 2)
                tt("dve", t1[:, :], R_t1, ps[ba][:, :], R_ps[ba], cos_t[:, :], R_cos, ALU.mult)
                tt("dve", t2[:, :], R_t2, ps[bb][:, :], R_ps[bb], sin_t[:, :], R_sin, ALU.mult)
                if j == 0:
                    tk.emit("dve", lambda p=p: nc.vector.memset(mid[64:128, 8 + p, :], 0.0), writes=[R_mid[8 + p]])
                    tk.emit("dve", lambda p=p: nc.vector.memset(mid[0:64, 12 + p, :], 0.0), writes=[R_mid[12 + p]])
                else:
                    tk.emit("pool", lambda p=p: nc.gpsimd.memset(mid[64:128, 8 + p, :], 0.0), writes=[R_mid[8 + p]])
                    tk.emit("pool", lambda p=p: nc.gpsimd.memset(mid[0:64, 12 + p, :], 0.0), writes=[R_mid[12 + p]])
                ropeng = "dve" if j == 0 else "pool"
                tt(ropeng, mid[0:64, 8 + p, :], R_mid[8 + p], t1[0:64, :], R_t1, t2[0:64, :], R_t2, ALU.add)
                tt(ropeng, mid[64:128, 12 + p, :], R_mid[12 + p], t1[64:128, :], R_t1, t2[64:128, :], R_t2, ALU.add)
            if j == 0:
                dump("qn", mid[:, 0:8, :], R_mid[0:8])
                dump("qr", mid[:, 8:12, :], R_mid[8:12])
                dump("krope", krope[:, 0:T], R_krope)

            nkc = 4 * (j + 1)
            n = T * (j + 1)
            LA = 2

            def kv_load(hh, lo=0, hi=None):
                hi = n if hi is None else hi
                sl = hh % 2
                blks = range(lo // T, (hi + T - 1) // T)
                dma("sp", kring[:, sl, lo:hi], R_kr[sl], Kc[hh, :, lo:hi], [R_Kcs[b_] for b_ in blks], ds_kl[sl])
                dma("sp", vring[:, sl, lo:hi], R_vr[sl], Vc[hh, :, lo:hi], [R_Vcs[b_] for b_ in blks], ds_vl[sl])

            seq = [(hh, kc) for hh in range(H) for kc in range(nkc)]
            pts = {}
            grp = {}
            den_started = {}
            den_q = []
            gcount = [0]
            acc32 = [t1, t2]
            R_acc32 = [R_t1, R_t2]

            def emit_S(i):
                hh, kc = seq[i]
                sl = hh % 2
                half = (hh // 4) * 64
                p = hh % 4
                c = kc - 4 * j
                q0 = 128 * c if c > 0 else 0
                sbk = next_mm_bank()
                mm(ps[sbk][:, q0:T], R_ps[sbk], kring[:, sl, kc * 128:(kc + 1) * 128], R_kr[sl],
                   mid[:, hh, q0:T], R_mid[hh], True, False)
                mm(ps[sbk][:, q0:T], R_ps[sbk], krope[:, kc * 128:(kc + 1) * 128], R_krope,
                   mid[:, 8 + hh, q0:T], R_mid[8 + hh], False, True)
                pi = 24 + (i % 4)
                pts[i] = (pi, q0)
                act(mid[:, pi, q0:T], R_mid[pi], ps[sbk][:, q0:T], R_ps[sbk], AF.Exp, scale=SCALE)
                if c >= 0:
                    if j == 0:
                        act(mid[64:128, pi, q0:q0 + 64], R_mid[pi], ps[sbk][64:128, q0:q0 + 64], R_ps[sbk],
                            AF.Copy, scale=0.0)
                    else:
                        tk.emit("pool", lambda pi=pi, q0=q0: nc.gpsimd.memset(mid[64:128, pi, q0:q0 + 64], 0.0),
                                writes=[R_mid[pi]])

            def emit_PV(i):
                hh, kc = seq[i]
                sl = hh % 2
                ob = 6 + (hh % 2)
                db = 4 + (hh % 2)
                pi, q0 = pts.pop(i)
                first = kc == 0
                last = kc == nkc - 1
                mm(ps[ob][:, q0:T], R_ps[ob], vring[:, sl, kc * 128:(kc + 1) * 128], R_vr[sl],
                   mid[:, pi, q0:T], R_mid[pi], first, last)

                def den_mm(rhs, r_rhs, q0_, last_, hh=hh, db=db):
                    st_ = not den_started.get(hh, False)
                    den_started[hh] = True
                    mm(ps[db][:, q0_:T], R_ps[db], ones[:, :], R_const, rhs, r_rhs, st_, last_)

                c = kc - 4 * j
                if c < 0:
                    pos = kc % 4
                    stg = grp.setdefault(hh, {})
                    if pos == 0:
                        stg["p0"] = pi
                    elif pos == 1:
                        g = gcount[0] % 2
                        stg["g"] = g
                        p0 = stg["p0"]
                        tt("dve", acc32[g][:, :], R_acc32[g], mid[:, p0, :], R_mid[p0], mid[:, pi, :], R_mid[pi], ALU.add)
                    elif pos == 2:
                        g = stg["g"]
                        tt("dve", acc32[g][:, :], R_acc32[g], acc32[g][:, :], R_acc32[g], mid[:, pi, :], R_mid[pi], ALU.add)
                    else:
                        g = stg["g"]
                        tt("dve", accb[:, g, :], R_accb[g], acc32[g][:, :], R_acc32[g], mid[:, pi, :], R_mid[pi], ALU.add)
                        gcount[0] += 1
                        den_q.append((i + 2, lambda g=g, den_mm=den_mm: den_mm(accb[:, g, :], R_accb[g], 0, False)))
                while den_q and (den_q[0][0] <= i or last):
                    den_q.pop(0)[1]()
                if c >= 0:
                    den_mm(mid[:, pi, q0:T], R_mid[pi], q0, last)
                if last:
                    act(rscr2[:, :], R_rscr2, ps[db][:, :], R_ps[db], AF.Ln)
                    act(rden[:, :], R_rden, rscr2[:, :], R_rscr2, AF.Exp, scale=-1.0)
                    tt("dve", mid[:, 16 + hh, :], R_mid[16 + hh], ps[ob][:, :], R_ps[ob], rden[:, :], R_rden, ALU.mult)
                    if hh + 2 < H:
                        kv_load(hh + 2)

            mark(f"t{j}:attn")
            kv_load(0, c0, n)
            kv_load(1, c0, n)
            for i in range(len(seq) + LA):
                if i < len(seq):
                    emit_S(i)
                if i - LA >= 0:
                    emit_PV(i - LA)
            if j == 0:
                dump("oT", mid[:, 16:24, :], R_mid[16:24])

            mark(f"t{j}:wo")
            for b in range(2):
                slot = wnext(("w", f"wo_{b}"))
                wv = wring[:, slot, :].rearrange("p (k m) -> p k m", k=8)
                for cc in range(4):
                    d = 4 * b + cc
                    bank = next_mm_bank()
                    for k in range(8):
                        mm(ps[bank][:, :], R_ps[bank], wv[:, k, cc * 128:(cc + 1) * 128], R_w[slot],
                           mid[:, 16 + k, :], R_mid[16 + k], k == 0, k == 7)
                    ts("dve", tA[:, d, :], R_tA[d], ps[bank][:, :], R_ps[bank], 1.0, None, ALU.mult)
                    act(sq[:, d, :], R_sq[d], ps[bank][:, :], R_ps[bank], AF.Square)
            mark(f"t{j}:post_mix1")
            postnorm("mix_post_g1", after=lambda c: hn_scale(c, 1))
            if j == 0:
                dump("h_mix1", h_t[:, :, :], R_h)
            mark(f"t{j}:ffn1")
            ffn(1, hook=(lambda: early_mixpre(j + 1)) if j + 1 < ntiles else None, pre_scaled=True)

            hb_, Rb_ = h_t, R_h
            stores.append(lambda hb_=hb_, Rb_=Rb_, c0=c0, j=j: dma(
                "sp", outT[:, c0:c0 + T].rearrange("(c p) t -> p c t", p=128), R_outs[j % 2],
                hb_[:, :, :], Rb_, ds_o[j % 2]))

        stopped = False
        try:
            for j in range(ntiles):
                tile_body(j)
        except _Stop:
            stopped = True
        run_pending()
        while stores:
            stores.pop(0)()
        assert stopped or st["next"] == len(sched), (st["next"], len(sched))
        toks = [R_outs[0].w, R_outs[1].w]
        for d_ in ds_dbg.values():
            if d_.val:
                toks.append((d_.key, d_.sem, d_.val, None))
        tk.wait_all("sp", toks)
        for en in ("pe", "act", "dve", "pool"):
            e = tk.engs[en]
            if e.count:
                tk.wait_all("sp", [(en, e.sem, e.count, en)])
        build_program.stats = (tk.ninst, tk.nwaits)
        build_program.marks = marks
    return nc


_CACHE = {}


def _prepare(inputs):
    inp = {k: np.asarray(v) for k, v in inputs.items()}
    wblocks, wnames = build_weight_blocks(inp)
    vecs = build_vecs(inp)
    cos2, sinS = rope_tables()
    return inp, wblocks, wnames, vecs, cos2, sinS


def kernel(**inputs):
    inp, wblocks, wnames, vecs, cos2, sinS = _prepare(inputs)
    nc = build_program(wblocks.shape[0], wnames)
    x = inp["x"]
    in_maps = []
    for b in range(NCORES):
        in_maps.append({
            "xT": np.ascontiguousarray(x[b].T),
            "w32": wblocks,
            "vecs": vecs,
            "cos2": cos2,
            "sinS": sinS,
        })
    res = run_bass_kernel_spmd(nc, in_maps, core_ids=list(range(NCORES)))
    out = np.stack([np.ascontiguousarray(np.asarray(r["outT"]).T) for r in res.results], 0)
    return out.astype(np.float32)
```

```python
import math
from contextlib import ExitStack

import numpy as np
import concourse.bass as bass
import concourse.mybir as mybir
from concourse.bass_utils import run_bass_kernel_spmd

F32 = mybir.dt.float32
BF16 = mybir.dt.bfloat16
ALU = mybir.AluOpType
AF = mybir.ActivationFunctionType

D = 1024
S = 4096
T = 512
NT = S // T
DFF = 4096
H = 8
CW = 31
HALO = CW - 1
QL = 384
KVL = 256
RMS_EPS = 1e-6
LN_EPS = 1e-5
SCALE = 192 ** -0.5
NCORES = 8
WBLK = 4096


def _proj_block(W, cols, kc):
    sub = W[:, cols].reshape(kc, 128, len(cols))
    return np.ascontiguousarray(sub.transpose(1, 0, 2)).reshape(128, kc * len(cols))


def _pad_block(b):
    out = np.zeros((128, WBLK), np.float32)
    out[:, : b.shape[1]] = b
    return out


def build_weight_blocks(inp):
    blocks = []
    names = {}

    def add(name, b):
        names[name] = len(blocks)
        blocks.append(_pad_block(b))

    ar = np.arange
    w1 = inp["conv_w_pw1"][0]
    for b in range(4):
        cols = np.concatenate([
            ar(128) + (2 * b) * 128, ar(128) + 1024 + (2 * b) * 128,
            ar(128) + (2 * b + 1) * 128, ar(128) + 1024 + (2 * b + 1) * 128])
        add(f"pw1_{b}", _proj_block(w1, cols, 8))
    w2 = inp["conv_w_pw2"][0]
    for b in range(2):
        add(f"pw2_{b}", _proj_block(w2, ar(512) + 512 * b, 8))
    def add_ffn(l):
        f1 = inp["w_ff1"][l]
        f2 = inp["w_ff2"][l]
        for b in range(8):
            add(f"ff1_{l}_{b}", _proj_block(f1, ar(512) + 512 * b, 8))
        for d in range(8):
            add(f"ff2_{l}_{d}", _proj_block(f2, ar(128) + 128 * d, 32))

    add_ffn(0)
    kr = inp["kv_w_kr"]
    sw = (ar(64) + 32) % 64
    kva = np.concatenate([inp["kv_w_dkv"], kr, kr, kr[:, sw], kr[:, sw]], axis=1)
    add("kva", _proj_block(kva, ar(512), 8))
    add("dq", _proj_block(inp["mla_w_dq"][0], ar(384), 8))
    kvb = np.concatenate([inp["kv_w_uk"], inp["kv_w_uv"]], axis=1)
    add("kvb", _proj_block(kvb, ar(2048), 2))
    uq = inp["mla_w_uq"][0]
    cols = []
    for h in range(8):
        cols.append(h * 192 + ar(128))
    for p in range(4):
        cols.append(p * 192 + 128 + ar(64))
        cols.append((p + 4) * 192 + 128 + ar(64))
    for p in range(4):
        cols.append(p * 192 + 128 + sw)
        cols.append((p + 4) * 192 + 128 + sw)
    cols = np.concatenate(cols)
    add("uq_0", _proj_block(uq, cols[:1024], 3))
    add("uq_1", _proj_block(uq, cols[1024:], 3))
    wo = inp["mla_w_o"][0]
    for b in range(2):
        add(f"wo_{b}", _proj_block(wo, ar(512) + 512 * b, 8))
    add_ffn(1)
    return np.stack(blocks, 0), names


VEC_SPECS = [("mix_pre_g0", 8), ("mix_pre_g1", 8), ("mix_post_g0", 8), ("mix_post_g1", 8),
             ("ffn_pre_g0", 8), ("ffn_pre_g1", 8), ("ffn_post_g0", 8), ("ffn_post_g1", 8),
             ("b_pw1", 16), ("w_dw", 8 * CW), ("b_dw", 8), ("ln_g", 8), ("ln_b", 8),
             ("b_pw2", 8), ("q_norm_g", 3), ("kv_in_g", 8), ("kv_norm_g", 2)]
VEC_OFF = {}
_o = 0
for _n, _c in VEC_SPECS:
    VEC_OFF[_n] = _o
    _o += _c
NVEC = _o


def build_vecs(inp):
    def colz(v):
        return np.ascontiguousarray(v.reshape(-1, 128).T)
    parts = {}
    for nm in ("mix_pre_g", "mix_post_g", "ffn_pre_g", "ffn_post_g"):
        for l in range(2):
            parts[f"{nm}{l}"] = colz(inp[nm][l])
    parts["b_pw1"] = colz(inp["conv_b_pw1"][0])
    wd = inp["conv_w_dw"][0]
    parts["w_dw"] = np.ascontiguousarray(
        wd.reshape(CW, 8, 128).transpose(2, 1, 0)).reshape(128, 8 * CW)
    parts["b_dw"] = colz(inp["conv_b_dw"][0])
    parts["ln_g"] = colz(inp["conv_ln_g"][0])
    parts["ln_b"] = colz(inp["conv_ln_b"][0])
    parts["b_pw2"] = colz(inp["conv_b_pw2"][0])
    parts["q_norm_g"] = colz(inp["mla_q_norm_g"][0])
    parts["kv_in_g"] = colz(inp["kv_in_g"])
    parts["kv_norm_g"] = colz(inp["kv_norm_g"])
    return np.ascontiguousarray(
        np.concatenate([parts[n] for n, _ in VEC_SPECS], axis=1).astype(np.float32))


def rope_tables():
    inv = (1.0 / (np.float32(10000.0) ** (np.arange(0, 64, 2, dtype=np.float32) / np.float32(64)))).astype(np.float32)
    pos = np.arange(S, dtype=np.float32)
    ang = (pos[:, None] * inv[None, :]).astype(np.float32)
    cos = np.cos(ang).astype(np.float32).T
    sin = np.sin(ang).astype(np.float32).T
    cos2 = np.concatenate([cos, cos, cos, cos], 0)
    sinS = np.concatenate([-sin, sin, -sin, sin], 0)
    return np.ascontiguousarray(cos2), np.ascontiguousarray(sinS)


class Res:
    __slots__ = ("name", "w", "r", "excl")

    def __init__(self, name, excl=False):
        self.name = name
        self.w = None
        self.r = {}
        self.excl = excl


class DSem:
    __slots__ = ("key", "sem", "val")

    def __init__(self, key, sem):
        self.key = key
        self.sem = sem
        self.val = 0


class _Eng:
    def __init__(self, name, h, sem):
        self.name = name
        self.h = h
        self.sem = sem
        self.count = 0
        self.seen = {}


class TK:
    def __init__(self):
        self.engs = {}
        self.nwaits = 0
        self.ninst = 0

    def add_engine(self, name, h, sem):
        self.engs[name] = _Eng(name, h, sem)

    def emit(self, eng, fn, reads=(), writes=(), signal=True, dsem=None):
        e = self.engs[eng]
        deps = {}

        def add(tok, same_ok):
            if tok is None:
                return
            key, sem, val, src = tok
            if src == eng and not same_ok:
                return
            cur = deps.get(key)
            if cur is None or cur[1] < val:
                deps[key] = (sem, val, src)

        for r in reads:
            add(r.w, True)
            if r.excl:
                for t in r.r.values():
                    add(t, False)
        for w in writes:
            add(w.w, False)
            for t in w.r.values():
                add(t, False)
        for key, (sem, val, src) in deps.items():
            if e.seen.get(key, 0) >= val:
                continue
            if src is not None:
                assert self.engs[src].count >= val, (eng, src, val, self.engs[src].count)
            e.h.wait_ge(sem, val)
            e.seen[key] = val
            self.nwaits += 1
        ins = fn()
        self.ninst += 1
        if dsem is not None:
            dsem.val += 16
            ins.then_inc(dsem.sem, 16)
            tok = (dsem.key, dsem.sem, dsem.val, None)
        elif signal:
            e.count += 1
            ins.then_inc(e.sem, 1)
            tok = (eng, e.sem, e.count, eng)
        else:
            tok = (eng, e.sem, e.count + 1, eng)
        for w in writes:
            w.w = tok
            w.r = {}
        for r in reads:
            cur = r.r.get(tok[0])
            if cur is None or cur[2] < tok[2]:
                r.r[tok[0]] = tok
        return tok

    def wait_all(self, eng, toks):
        e = self.engs[eng]
        for tok in toks:
            if tok is None:
                continue
            key, sem, val, src = tok
            if e.seen.get(key, 0) >= val:
                continue
            e.h.wait_ge(sem, val)
            e.seen[key] = val


class _Stop(Exception):
    pass


def build_program(nblk, wnames, ntiles=NT, dumps=(), skip_cast=False, stop_after=None):
    nc = bass.Bass("TRN2", target_bir_lowering=False)
    xT = nc.dram_tensor("xT", [D, S], F32, kind="ExternalInput").ap()
    w32 = nc.dram_tensor("w32", [nblk, 128, WBLK], F32, kind="ExternalInput").ap()
    vecs_d = nc.dram_tensor("vecs", [128, NVEC], F32, kind="ExternalInput").ap()
    cos_d = nc.dram_tensor("cos2", [128, S], F32, kind="ExternalInput").ap()
    sin_d = nc.dram_tensor("sinS", [128, S], F32, kind="ExternalInput").ap()
    outT = nc.dram_tensor("outT", [D, S], F32, kind="ExternalOutput").ap()
    wbf = nc.dram_tensor("wbf", [nblk, 128, WBLK], BF16, kind="Internal").ap()
    Kc = nc.dram_tensor("Kc", [H, 128, S], BF16, kind="Internal").ap()
    diagbf = nc.dram_tensor("diagbf", [8, 128, WBLK], BF16, kind="Internal").ap()
    Vc = nc.dram_tensor("Vc", [H, 128, S], BF16, kind="Internal").ap()
    dump_aps = {}
    for nm, shape, dt in dumps:
        dump_aps[nm] = nc.dram_tensor("dbg_" + nm, list(shape), dt, kind="ExternalOutput").ap()

    es = ExitStack()
    with es:
        def sb(name, shape, dt):
            return es.enter_context(nc.sbuf_tensor(name, list(shape), dt))

        def sem(name):
            return es.enter_context(nc.semaphore(name))

        tk = TK()
        tk.add_engine("pe", nc.tensor, sem("s_pe"))
        tk.add_engine("act", nc.scalar, sem("s_act"))
        tk.add_engine("dve", nc.vector, sem("s_dve"))
        tk.add_engine("pool", nc.gpsimd, sem("s_pool"))
        tk.add_engine("sp", nc.sync, sem("s_sp"))

        h_a = sb("h", [128, 8, T], F32)
        h_b = sb("h2", [128, 8, T], F32)
        h_t = h_a
        tA = sb("tA", [128, 8, T], F32)
        hn = sb("hn", [128, 8, T], BF16)
        sq = sb("sq", [128, 8, T], BF16)
        mid = sb("mid", [128, 32, T], BF16)
        ubuf = sb("ubuf", [128, 8, HALO + T], BF16)
        sig = sb("sig", [128, 2, T], F32)
        wring = sb("wring", [128, 3, WBLK], BF16)
        rstd = sb("rstd", [128, T], F32)
        mean = sb("mean", [128, T], F32)
        t1 = sb("t1", [128, T], F32)
        t2 = sb("t2", [128, T], F32)
        rden = sb("rden", [128, T], F32)
        rstd2 = sb("rstd2", [128, T], F32)
        accb = sb("accb", [128, 2, T], BF16)
        rscr = sb("rscr", [128, T], F32)
        rscr2 = sb("rscr2", [128, T], F32)
        ckv = sb("ckv", [128, 2, T], BF16)
        cq = sb("cq", [128, 3, T], BF16)
        krope = sb("krope", [128, S], BF16)
        cos_t = sb("cos_t", [128, T], F32)
        sin_t = sb("sin_t", [128, T], F32)
        kring = sb("kring", [128, 2, S], BF16)
        vring = sb("vring", [128, 2, S], BF16)
        vecs = sb("vecs_sb", [128, NVEC], F32)
        ones = sb("ones", [128, 128], BF16)
        ident = sb("ident", [128, 128], BF16)
        epsc = sb("epsc", [128, 2], F32)
        identf = sb("identf", [128, 128], F32)

        ps = [es.enter_context(nc.psum_tensor(f"ps{i}", [128, T], F32)) for i in range(8)]

        R_ha = [Res(f"h{c}") for c in range(8)]
        R_hb = [Res(f"hb{c}") for c in range(8)]
        R_h = R_ha
        hbufs = [(h_a, R_ha), (h_b, R_hb)]
        R_tA = [Res(f"tA{c}") for c in range(8)]
        R_hn = [Res(f"hn{c}") for c in range(8)]
        R_sq = [Res(f"sq{c}") for c in range(8)]
        R_mid = [Res(f"mid{c}") for c in range(32)]
        R_ub = [Res(f"ub{c}") for c in range(8)]
        R_sig = [Res("sig0"), Res("sig1")]
        R_w = [Res(f"w{i}") for i in range(3)]
        R_rstd, R_mean, R_t1, R_t2, R_rden = Res("rstd"), Res("mean"), Res("t1"), Res("t2"), Res("rden")
        R_rscr, R_rscr2 = Res("rscr"), Res("rscr2")
        R_rstd2 = Res("rstd2")
        R_accb = [Res("accb0"), Res("accb1")]

        R_ckv = [Res("ckv0"), Res("ckv1")]
        R_cq = [Res(f"cq{i}") for i in range(3)]
        R_krope = Res("krope")
        R_cos, R_sin = Res("cos"), Res("sin")
        R_kr = [Res("kr0"), Res("kr1")]
        R_vr = [Res("vr0"), Res("vr1")]
        R_const = Res("const")
        R_ps = [Res(f"ps{i}", excl=True) for i in range(8)]
        R_wbf = [Res(f"wbf{i}") for i in range(nblk)]
        R_Kcs = [Res(f"Kc{i}") for i in range(NT)]
        R_Vcs = [Res(f"Vc{i}") for i in range(NT)]
        R_outs = [Res("out0"), Res("out1")]

        ds_w = [DSem(f"dw{i}", sem(f"d_w{i}")) for i in range(3)]
        ds_x = [DSem("dx0", sem("d_x0")), DSem("dx1", sem("d_x1"))]
        ds_o = [DSem("do0", sem("d_o0")), DSem("do1", sem("d_o1"))]
        ds_c = DSem("dc", sem("d_c"))
        ds_cos = DSem("dcos", sem("d_cos"))
        ds_sin = DSem("dsin", sem("d_sin"))
        ds_ks = DSem("dks", sem("d_ks"))
        ds_vs = DSem("dvs", sem("d_vs"))
        ds_kl = [DSem(f"dkl{i}", sem(f"d_kl{i}")) for i in range(2)]
        ds_vl = [DSem(f"dvl{i}", sem(f"d_vl{i}")) for i in range(2)]
        ds_pro = [DSem(f"dpro{i}", sem(f"d_pro{i}")) for i in range(4)]
        ds_dbg = {nm: DSem("ddbg_" + nm, sem("d_dbg_" + nm)) for nm, _, _ in dumps}

        def V(name, i=0):
            o = VEC_OFF[name] + i
            return vecs[:, o:o + 1]

        marks = []
        mmc = [0]

        def mark(label):
            marks.append((mmc[0], label))

        def mm(out, r_out, lhsT, r_l, rhs, r_r, start, stop, signal=None):
            mmc[0] += 1
            if signal is None:
                signal = stop
            rd = [r_l, r_r] if isinstance(r_r, Res) else [r_l] + list(r_r)
            return tk.emit("pe", lambda: nc.tensor.matmul(out, lhsT, rhs, start=start, stop=stop),
                           reads=rd, writes=[r_out], signal=signal)

        def act(out, r_out, in_, r_in, func, bias=None, scale=None, extra_reads=()):
            kw = {}
            if bias is not None:
                kw["bias"] = bias
            if scale is not None:
                kw["scale"] = scale
            rins = r_in if isinstance(r_in, (list, tuple)) else [r_in]
            return tk.emit("act", lambda: nc.scalar.activation(out, in_, func, **kw),
                           reads=list(rins) + [R_const] + list(extra_reads), writes=[r_out])

        def ts(eng, out, r_out, in0, r_in, s1, s2, op0, op1=None):
            h = nc.vector if eng == "dve" else nc.gpsimd
            if op1 is None:
                f = lambda: h.tensor_scalar(out, in0, s1, None, op0)
            else:
                f = lambda: h.tensor_scalar(out, in0, s1, s2, op0, op1)
            return tk.emit(eng, f, reads=[r_in, R_const], writes=[r_out])

        def tt(eng, out, r_out, in0, r0, in1, r1, op):
            h = nc.vector if eng == "dve" else nc.gpsimd
            return tk.emit(eng, lambda: h.tensor_tensor(out, in0, in1, op),
                           reads=[r0, r1], writes=[r_out])

        def stt(out, r_out, in0, r0, scalar, in1, r1, op0, op1):
            return tk.emit("dve", lambda: nc.vector.scalar_tensor_tensor(out, in0, scalar, in1, op0, op1),
                           reads=[r0, r1, R_const], writes=[r_out])

        def dma(eng, out, r_out, in_, r_in, dsem, **kw):
            h = nc.sync if eng == "sp" else nc.gpsimd
            rd = [r_in] if isinstance(r_in, Res) else list(r_in)
            wr = [r_out] if isinstance(r_out, Res) else list(r_out)
            return tk.emit(eng, lambda: h.dma_start(out=out, in_=in_, **kw), reads=rd, writes=wr, dsem=dsem)

        def eps_ap(eps):
            return epsc[:, 0:1] if eps == RMS_EPS else epsc[:, 1:2]

        tk.emit("pool", lambda: nc.gpsimd.memset(epsc[:, 0:1], RMS_EPS), writes=[R_const])
        tk.emit("pool", lambda: nc.gpsimd.memset(epsc[:, 1:2], LN_EPS), writes=[R_const])
        dma("sp", vecs[:, :], R_const, vecs_d[:, :], Res("vecs_d"), ds_c)
        tk.emit("pool", lambda: nc.gpsimd.memset(ones[:, :], 1.0), writes=[R_const])
        tk.emit("pool", lambda: nc.gpsimd.memset(identf[:, :], 0.0), writes=[R_const])
        tk.emit("pool", lambda: nc.gpsimd.affine_select(
            identf[:, :], identf[:, :], [[-1, 128]], ALU.not_equal, 1.0, base=0, channel_multiplier=1),
            reads=[R_const], writes=[R_const])
        tk.emit("pool", lambda: nc.gpsimd.tensor_copy(ident[:, :], identf[:, :]),
                reads=[R_const], writes=[R_const])
        tk.emit("pool", lambda: nc.gpsimd.memset(ubuf[:, :, :], 0.0), writes=R_ub)

        R_diag = [Res(f"diag{c}") for c in range(8)]
        ds_diag = [DSem(f"ddg{c}", sem(f"d_dg{c}")) for c in range(8)]

        def gen_diags():
            for c in range(8):
                q4 = c % 4
                stage = mid[:, 8 * q4:8 * q4 + 8, :].rearrange("p a t -> p (a t)")
                dst = stage[:, 0:CW * 128].rearrange("p (k m) -> p k m", k=CW)
                o = VEC_OFF["w_dw"] + c * CW
                rr = R_mid[8 * q4:8 * q4 + 8]
                tk.emit("pool", lambda dst=dst, o=o: nc.gpsimd.tensor_tensor(
                    dst, ident[:, :].unsqueeze(1).broadcast_to([128, CW, 128]),
                    vecs[:, o:o + CW].unsqueeze(2).broadcast_to([128, CW, 128]), ALU.mult),
                    reads=[R_const], writes=rr)
                tk.emit("pool", lambda stage=stage: nc.gpsimd.memset(stage[:, CW * 128:WBLK], 0.0), writes=rr)
                dma("pool", diagbf[c, :, :], R_diag[c], stage[:, :], rr, ds_diag[c])

        if not skip_cast:
            lanes = [Res(f"pro_lane{i}") for i in range(4)]

            def cast_blk(b):
                tk.emit("pool", lambda b=b: nc.gpsimd.dma_start(
                    out=wbf[b], in_=w32[b], max_dma_last_dim=2048 * 4),
                    reads=[], writes=[R_wbf[b], lanes[b % 4]], dsem=ds_pro[b % 4])

            for b in range(min(4, nblk)):
                cast_blk(b)
            gen_diags()
            for b in range(4, nblk):
                cast_blk(b)
        else:
            gen_diags()

        sched = []
        for j in range(ntiles):
            sched += [("w", "pw1_0"), ("w", "pw1_1"), ("diag", 0), ("diag", 1), ("w", "pw1_2"),
                      ("diag", 2), ("diag", 3), ("w", "pw1_3"), ("diag", 4), ("diag", 5),
                      ("diag", 6), ("diag", 7), ("w", "pw2_0"), ("w", "pw2_1")]
            sched += [("w", f"ff1_0_{b}") for b in range(8)]
            sched += [("w", f"ff2_0_{b}") for b in range(8)]
            sched += [("w", "kva"), ("w", "dq"), ("w", "kvb"), ("w", "uq_0"), ("w", "uq_1"),
                      ("w", "wo_0"), ("w", "wo_1")]
            sched += [("w", f"ff1_1_{b}") for b in range(8)]
            sched += [("w", f"ff2_1_{b}") for b in range(8)]
        st = {"issued": 0, "next": 0}

        def issue_one():
            i = st["issued"]
            if i >= len(sched):
                return
            slot = i % 3
            kind, arg = sched[i]
            if kind == "w":
                b = wnames[arg]
                dma("sp", wring[:, slot, :], R_w[slot], wbf[b], R_wbf[b], ds_w[slot])
            else:
                c = arg
                dma("sp", wring[:, slot, :], R_w[slot], diagbf[c], R_diag[c], ds_w[slot])
            st["issued"] += 1

        def wnext(expect):
            i = st["next"]
            assert sched[i] == expect, (sched[i], expect)
            while st["issued"] < min(len(sched), i + 3):
                issue_one()
            st["next"] += 1
            return i % 3

        pscur = {"mm": 0}

        def next_mm_bank():
            b = pscur["mm"]
            pscur["mm"] = (b + 1) % 4
            return b

        def stats(srcs, bank):
            n = len(srcs)
            for i, (ap, r) in enumerate(srcs):
                mm(ps[bank][:, :], R_ps[bank], ones[:, :], R_const, ap, r, i == 0, i == n - 1)

        def rstd_from(bank, dim, eps, out=None, r_out=None, scr=None, r_scr=None):
            out = rstd if out is None else out
            r_out = R_rstd if r_out is None else r_out
            scr = rscr if scr is None else scr
            r_scr = R_rscr if r_scr is None else r_scr
            act(scr[:, :], r_scr, ps[bank][:, :], R_ps[bank], AF.Ln, bias=eps_ap(eps), scale=1.0 / dim)
            act(out[:, :], r_out, scr[:, :], r_scr, AF.Exp, scale=-0.5)
            return

        def _rstd_from_old(bank, dim, eps):
            act(rscr[:, :], R_rscr, ps[bank][:, :], R_ps[bank], AF.Ln, bias=eps_ap(eps), scale=1.0 / dim)
            act(rstd[:, :], R_rstd, rscr[:, :], R_rscr, AF.Exp, scale=-0.5)

        def rms_stats(src, r_src, nch, dim):
            dump("ck_load", None, None)
            for c in range(nch):
                act(sq[:, c, :], R_sq[c], src[:, c, :], r_src[c], AF.Square)
            dump("ck_sq", None, None)
            stats([(sq[:, c, :], R_sq[c]) for c in range(nch)], 4)
            dump("ck_stats", None, None)
            rstd_from(4, dim, RMS_EPS)
            dump("ck_rstd", None, None)

        def prenorm(gname, src=None, r_src=None, dst=None, r_dst=None, nch=8, rs=None, r_rs=None):
            src = h_t if src is None else src
            r_src = R_h if r_src is None else r_src
            dst = hn if dst is None else dst
            r_dst = R_hn if r_dst is None else r_dst
            rs = rstd if rs is None else rs
            r_rs = R_rstd if r_rs is None else r_rs
            for c in range(nch):
                stt(dst[:, c, :], r_dst[c], src[:, c, :], r_src[c], V(gname, c), rs[:, :], r_rs,
                    ALU.mult, ALU.mult)

        def early_mixpre(jn):
            hb, Rb = hbufs[jn % 2]
            for c in range(8):
                act(hn[:, c, :], R_hn[c], hb[:, c, :], Rb[c], AF.Square)
            stats([(hn[:, c, :], R_hn[c]) for c in range(8)], 5)
            rstd_from(5, D, RMS_EPS, out=rstd2, r_out=R_rstd2, scr=rscr2, r_scr=R_rscr2)
            prenorm("mix_pre_g0", src=hb, r_src=Rb, rs=rstd2, r_rs=R_rstd2)

        stores = []
        pending = []

        def run_pending(n=None):
            k = len(pending) if n is None else min(n, len(pending))
            for _ in range(k):
                pending.pop(0)()

        def postnorm(gname, defer=False, after=None):
            stats([(sq[:, c, :], R_sq[c]) for c in range(8)], 4)
            rstd_from(4, D, RMS_EPS)
            hb, Rb = h_t, R_h

            def apply(c):
                stt(tA[:, c, :], R_tA[c], tA[:, c, :], R_tA[c], V(gname, c), rstd[:, :], R_rstd,
                    ALU.mult, ALU.mult)
                tt("dve", hb[:, c, :], Rb[c], hb[:, c, :], Rb[c], tA[:, c, :], R_tA[c], ALU.add)
                if after is not None:
                    after(c)

            for c in range(8):
                if defer:
                    pending.append(lambda c=c: apply(c))
                else:
                    apply(c)

        def proj8(blockname, nout, evac, kouter=False, mid_hook=None):
            slot = wnext(("w", blockname))
            wv = wring[:, slot, :].rearrange("p (k m) -> p k m", k=8)
            if kouter:
                banks = [next_mm_bank() for _ in range(nout)]
                for k in range(8):
                    for cc in range(nout):
                        mm(ps[banks[cc]][:, :], R_ps[banks[cc]], wv[:, k, cc * 128:(cc + 1) * 128], R_w[slot],
                           hn[:, k, :], R_hn[k], k == 0, k == 7)
                if mid_hook is not None:
                    mid_hook()
                for cc in range(nout):
                    evac(cc, banks[cc])
                return
            for cc in range(nout):
                bank = next_mm_bank()
                for k in range(8):
                    mm(ps[bank][:, :], R_ps[bank], wv[:, k, cc * 128:(cc + 1) * 128], R_w[slot],
                       hn[:, k, :], R_hn[k], k == 0, k == 7)
                evac(cc, bank)

        def hn_scale(c, l):
            ts("dve", hn[:, c, :], R_hn[c], h_t[:, c, :], R_h[c], V(f"ffn_pre_g{l}", c), None, ALU.mult)

        def ffn(l, hook=None, pre_scaled=False):
            mark(f"ffn{l}:prenorm")
            if not pre_scaled:
                for c in range(8):
                    hn_scale(c, l)
            for c in range(8):
                act(sq[:, c, :], R_sq[c], h_t[:, c, :], R_h[c], AF.Square)

            def stats_hook():
                stats([(sq[:, c, :], R_sq[c]) for c in range(8)], 4)
                rstd_from(4, D, RMS_EPS)

            for b in range(8):
                def ev(cc, bank, b=b):
                    m = 4 * b + cc
                    stt(mid[:, m, :], R_mid[m], ps[bank][:, :], R_ps[bank], 0.0, rstd[:, :], R_rstd,
                        ALU.max, ALU.mult)
                    tt("dve", mid[:, m, :], R_mid[m], mid[:, m, :], R_mid[m], mid[:, m, :], R_mid[m], ALU.mult)
                proj8(f"ff1_{l}_{b}", 4, ev, kouter=(b == 0), mid_hook=stats_hook if b == 0 else None)
            mark(f"ffn{l}:ff2")
            for d in range(8):
                slot = wnext(("w", f"ff2_{l}_{d}"))
                wv = wring[:, slot, :].rearrange("p (k m) -> p k m", k=32)
                bank = next_mm_bank()
                for k in range(32):
                    mm(ps[bank][:, :], R_ps[bank], wv[:, k, :], R_w[slot], mid[:, k, :], R_mid[k],
                       k == 0, k == 31)
                ts("dve", tA[:, d, :], R_tA[d], ps[bank][:, :], R_ps[bank], 1.0, None, ALU.mult)
                act(sq[:, d, :], R_sq[d], ps[bank][:, :], R_ps[bank], AF.Square)
                if d == 1 and hook is not None:
                    hook()
            mark(f"ffn{l}:post")
            postnorm(f"ffn_post_g{l}", defer=(l == 1))

        def dump(nm, ap, rs):
            if nm in dump_aps:
                dma("sp", dump_aps[nm], Res("dbg_" + nm), ap, rs, ds_dbg[nm])
            if stop_after == nm:
                raise _Stop()

        def tile_body(j):
            nonlocal h_t, R_h
            c0 = j * T
            h_t, R_h = hbufs[j % 2]
            if j == 0:
                dma("sp", h_t[:, :, :], R_h, xT[:, 0:T].rearrange("(c p) t -> p c t", p=128),
                    Res("x_d"), ds_x[0])
            dma("sp", cos_t[:, :], R_cos, cos_d[:, c0:c0 + T], Res("cos_d"), ds_cos)
            dma("sp", sin_t[:, :], R_sin, sin_d[:, c0:c0 + T], Res("sin_d"), ds_sin)

            mark(f"t{j}:mixpre0")
            if j == 0:
                rms_stats(h_t, R_h, 8, D)
                prenorm("mix_pre_g0")
            if j == 0:
                dump("hn0", hn[:, :, :], R_hn)

            def glu_block(b):
                slot = wnext(("w", f"pw1_{b}"))
                wv = wring[:, slot, :].rearrange("p (k m) -> p k m", k=8)
                if b == 0:
                    gb = [next_mm_bank() for _ in range(4)]
                    for k in range(8):
                        for g4 in range(4):
                            mm(ps[gb[g4]][:, :], R_ps[gb[g4]], wv[:, k, g4 * 128:(g4 + 1) * 128], R_w[slot],
                               hn[:, k, :], R_hn[k], k == 0, k == 7)
                for i in range(2):
                    c = 2 * b + i
                    if b == 0:
                        ba, bg = gb[2 * i], gb[2 * i + 1]
                    else:
                        ba = next_mm_bank()
                        for k in range(8):
                            mm(ps[ba][:, :], R_ps[ba], wv[:, k, (2 * i) * 128:(2 * i + 1) * 128], R_w[slot],
                               hn[:, k, :], R_hn[k], k == 0, k == 7)
                        bg = next_mm_bank()
                        for k in range(8):
                            mm(ps[bg][:, :], R_ps[bg], wv[:, k, (2 * i + 1) * 128:(2 * i + 2) * 128], R_w[slot],
                               hn[:, k, :], R_hn[k], k == 0, k == 7)
                    si = c % 2
                    act(sig[:, si, :], R_sig[si], ps[bg][:, :], R_ps[bg], AF.Sigmoid, bias=V("b_pw1", 8 + c))
                    stt(ubuf[:, c, HALO:HALO + T], R_ub[c], ps[ba][:, :], R_ps[ba], V("b_pw1", c),
                        sig[:, si, :], R_sig[si], ALU.add, ALU.mult)
                    run_pending(1)

            def conv_chunk(c):
                slot = wnext(("diag", c))
                wv = wring[:, slot, 0:CW * 128].rearrange("p (k m) -> p k m", k=CW)
                bank = next_mm_bank()
                for k in range(CW):
                    mm(ps[bank][:, :], R_ps[bank], wv[:, k, :], R_w[slot], ubuf[:, c, k:k + T], R_ub[c],
                       k == 0, k == CW - 1)
                act(tA[:, c, :], R_tA[c], ps[bank][:, :], R_ps[bank], AF.Identity, bias=V("b_dw", c))
                act(mid[:, c, :], R_mid[c], ps[bank][:, :], R_ps[bank], AF.Identity, bias=V("b_dw", c))
                act(sq[:, c, :], R_sq[c], ps[bank][:, :], R_ps[bank], AF.Square, bias=V("b_dw", c))
                tk.emit("pool", lambda: nc.gpsimd.tensor_copy(ubuf[:, c, 0:HALO], ubuf[:, c, T:T + HALO]),
                        reads=[R_ub[c]], writes=[R_ub[c]])

            mark(f"t{j}:pw1conv")
            glu_block(0)
            glu_block(1)
            conv_chunk(0)
            conv_chunk(1)
            glu_block(2)
            conv_chunk(2)
            conv_chunk(3)
            glu_block(3)
            run_pending()
            while stores:
                stores.pop(0)()
            for c in range(4, 8):
                conv_chunk(c)
            if j == 0:
                dump("conv0", tA[:, :, :], R_tA)

            mark(f"t{j}:LN")
            stats([(mid[:, c, :], R_mid[c]) for c in range(8)], 4)
            stats([(sq[:, c, :], R_sq[c]) for c in range(8)], 5)
            ts("dve", mean[:, :], R_mean, ps[4][:, :], R_ps[4], 1.0 / D, None, ALU.mult)
            tt("dve", t1[:, :], R_t1, mean[:, :], R_mean, mean[:, :], R_mean, ALU.mult)
            stt(t2[:, :], R_t2, ps[5][:, :], R_ps[5], 1.0 / D, t1[:, :], R_t1, ALU.mult, ALU.subtract)
            act(rscr[:, :], R_rscr, t2[:, :], R_t2, AF.Ln, bias=eps_ap(LN_EPS), scale=1.0)
            act(rstd[:, :], R_rstd, rscr[:, :], R_rscr, AF.Exp, scale=-0.5)
            for c in range(8):
                tt("dve", tA[:, c, :], R_tA[c], tA[:, c, :], R_tA[c], mean[:, :], R_mean, ALU.subtract)
                tt("dve", tA[:, c, :], R_tA[c], tA[:, c, :], R_tA[c], rstd[:, :], R_rstd, ALU.mult)
                act(hn[:, c, :], R_hn[c], tA[:, c, :], R_tA[c], AF.Silu, bias=V("ln_b", c), scale=V("ln_g", c))
            if j == 0:
                dump("lnsilu0", hn[:, :, :], R_hn)

            for b in range(2):
                def ev(cc, bank, b=b):
                    d = 4 * b + cc
                    act(tA[:, d, :], R_tA[d], ps[bank][:, :], R_ps[bank], AF.Identity, bias=V("b_pw2", d))
                    act(sq[:, d, :], R_sq[d], ps[bank][:, :], R_ps[bank], AF.Square, bias=V("b_pw2", d))
                proj8(f"pw2_{b}", 4, ev, kouter=(b == 0))
            mark(f"t{j}:post_mix0")
            postnorm("mix_post_g0", after=lambda c: hn_scale(c, 0))
            if j == 0:
                dump("h_mix0", h_t[:, :, :], R_h)
            if j + 1 < ntiles:
                hn_, Rn_ = hbufs[(j + 1) % 2]
                dma("sp", hn_[:, :, :], Rn_, xT[:, c0 + T:c0 + 2 * T].rearrange("(c p) t -> p c t", p=128),
                    Res("x_d"), ds_x[(j + 1) % 2])
            mark(f"t{j}:ffn0")
            ffn(0, pre_scaled=True)
            if j == 0:
                dump("h_l0", h_t[:, :, :], R_h)

            mark(f"t{j}:kvnorm")
            if j > 0:
                for hh_ in (0, 1):
                    sl_ = hh_ % 2
                    dma("sp", kring[:, sl_, 0:c0], R_kr[sl_], Kc[hh_, :, 0:c0], R_Kcs[0:j], ds_kl[sl_])
                    dma("sp", vring[:, sl_, 0:c0], R_vr[sl_], Vc[hh_, :, 0:c0], R_Vcs[0:j], ds_vl[sl_])
            rms_stats(h_t, R_h, 8, D)
            prenorm("kv_in_g")
            prenorm("mix_pre_g1", dst=mid, r_dst=R_mid)
            slot = wnext(("w", "kva"))
            wv = wring[:, slot, :].rearrange("p (k m) -> p k m", k=8)
            kb = [next_mm_bank() for _ in range(4)]
            for k in range(8):
                for g4 in range(4):
                    mm(ps[kb[g4]][:, :], R_ps[kb[g4]], wv[:, k, g4 * 128:(g4 + 1) * 128], R_w[slot],
                       hn[:, k, :], R_hn[k], k == 0, k == 7)
            for cc in range(2):
                ts("dve", tA[:, cc, :], R_tA[cc], ps[kb[cc]][:, :], R_ps[kb[cc]], 1.0, None, ALU.mult)
                act(sq[:, cc, :], R_sq[cc], ps[kb[cc]][:, :], R_ps[kb[cc]], AF.Square)
            bka, bkb = kb[2], kb[3]
            tt("dve", t1[:, :], R_t1, ps[bka][:, :], R_ps[bka], cos_t[:, :], R_cos, ALU.mult)
            tt("dve", t2[:, :], R_t2, ps[bkb][:, :], R_ps[bkb], sin_t[:, :], R_sin, ALU.mult)
            tt("dve", krope[:, c0:c0 + T], R_krope, t1[:, :], R_t1, t2[:, :], R_t2, ALU.add)
            mark(f"t{j}:dq")
            slot = wnext(("w", "dq"))
            wv = wring[:, slot, 0:8 * QL].rearrange("p (k m) -> p k m", k=8)
            qb = [next_mm_bank() for _ in range(3)]
            for k in range(8):
                for cc in range(3):
                    mm(ps[qb[cc]][:, :], R_ps[qb[cc]], wv[:, k, cc * 128:(cc + 1) * 128], R_w[slot],
                       mid[:, k, :], R_mid[k], k == 0, k == 7)
            for cc in range(3):
                ts("dve", tA[:, 2 + cc, :], R_tA[2 + cc], ps[qb[cc]][:, :], R_ps[qb[cc]], 1.0, None, ALU.mult)
                act(sq[:, 2 + cc, :], R_sq[2 + cc], ps[qb[cc]][:, :], R_ps[qb[cc]], AF.Square)
            stats([(sq[:, cc, :], R_sq[cc]) for cc in range(2)], 5)
            rstd_from(5, KVL, RMS_EPS, out=rstd2, r_out=R_rstd2, scr=rscr2, r_scr=R_rscr2)
            for cc in range(2):
                stt(ckv[:, cc, :], R_ckv[cc], tA[:, cc, :], R_tA[cc], V("kv_norm_g", cc), rstd2[:, :], R_rstd2,
                    ALU.mult, ALU.mult)
            stats([(sq[:, 2 + cc, :], R_sq[2 + cc]) for cc in range(3)], 4)
            rstd_from(4, QL, RMS_EPS)
            for cc in range(3):
                stt(cq[:, cc, :], R_cq[cc], tA[:, 2 + cc, :], R_tA[2 + cc], V("q_norm_g", cc), rstd[:, :], R_rstd,
                    ALU.mult, ALU.mult)
            if j == 0:
                dump("ckv", ckv[:, :, :], R_ckv)
            mark(f"t{j}:kvb")
            slot = wnext(("w", "kvb"))
            wv = wring[:, slot, :].rearrange("p (k m) -> p k m", k=2)
            for hh in range(H):
                bank = next_mm_bank()
                for k in range(2):
                    mm(ps[bank][:, :], R_ps[bank], wv[:, k, hh * 128:(hh + 1) * 128], R_w[slot],
                       ckv[:, k, :], R_ckv[k], k == 0, k == 1)
                act(mid[:, 16 + hh, :], R_mid[16 + hh], ps[bank][:, :], R_ps[bank], AF.Identity)
            dump("ck_kn", None, None)
            for tb in range(4):
                for half in range(2):
                    bank = next_mm_bank()
                    for k in range(2):
                        mm(ps[bank][:, :], R_ps[bank], ckv[:, k, tb * 128:(tb + 1) * 128], R_ckv[k],
                           wv[:, k, 1024 + half * 512:1024 + (half + 1) * 512], R_w[slot], k == 0, k == 1)
                    dst = mid[:, 24 + 4 * half:24 + 4 * half + 4, tb * 128:(tb + 1) * 128]
                    src = ps[bank][:, :].rearrange("p (h d) -> p h d", h=4)
                    rw = R_mid[24 + 4 * half:24 + 4 * half + 4]
                    if half == 0:
                        tk.emit("dve", lambda: nc.vector.tensor_copy(dst, src), reads=[R_ps[bank]], writes=rw)
                    else:
                        tk.emit("act", lambda: nc.scalar.copy(dst, src), reads=[R_ps[bank]], writes=rw)
            dump("ck_v", None, None)
            dma("sp", Kc[:, :, c0:c0 + T].rearrange("h p t -> p h t"), R_Kcs[j], mid[:, 16:24, :], R_mid[16:24], ds_ks)
            dma("sp", Vc[:, :, c0:c0 + T].rearrange("h p t -> p h t"), R_Vcs[j], mid[:, 24:32, :], R_mid[24:32], ds_vs)

            dump("ck_kvst", None, None)
            dump("ck_dq", None, None)
            mark(f"t{j}:uq")
            slot = wnext(("w", "uq_0"))
            wv = wring[:, slot, 0:3 * 1024].rearrange("p (k m) -> p k m", k=3)
            for hh in range(H):
                bank = next_mm_bank()
                for k in range(3):
                    mm(ps[bank][:, :], R_ps[bank], wv[:, k, hh * 128:(hh + 1) * 128], R_w[slot],
                       cq[:, k, :], R_cq[k], k == 0, k == 2)
                if hh % 2 == 0:
                    ts("dve", mid[:, hh, :], R_mid[hh], ps[bank][:, :], R_ps[bank], 1.0, None, ALU.mult)
                else:
                    act(mid[:, hh, :], R_mid[hh], ps[bank][:, :], R_ps[bank], AF.Identity)
            dump("ck_uq0", None, None)
            slot = wnext(("w", "uq_1"))
            wv = wring[:, slot, 0:3 * 1024].rearrange("p (k m) -> p k m", k=3)
            for p in range(4):
                ba = next_mm_bank()
                for k in range(3):
                    mm(ps[ba][:, :], R_ps[ba], wv[:, k, p * 128:(p + 1) * 128], R_w[slot],
                       cq[:, k, :], R_cq[k], k == 0, k == 2)
                bb = next_mm_bank()
                for k in range(3):
                    mm(ps[bb][:, :], R_ps[bb], wv[:, k, (4 + p) * 128:(5 + p) * 128], R_w[slot],
                       cq[:, k, :], R_cq[k], k == 0, k == 2)
                tt("dve", t1[:, :], R_t1, ps[ba][:, :], R_ps[ba], cos_t[:, :], R_cos, ALU.mult)
                tt("dve", t2[:, :], R_t2, ps[bb][:, :], R_ps[bb], sin_t[:, :], R_sin, ALU.mult)
                if j == 0:
                    tk.emit("dve", lambda p=p: nc.vector.memset(mid[64:128, 8 + p, :], 0.0), writes=[R_mid[8 + p]])
                    tk.emit("dve", lambda p=p: nc.vector.memset(mid[0:64, 12 + p, :], 0.0), writes=[R_mid[12 + p]])
                else:
                    tk.emit("pool", lambda p=p: nc.gpsimd.memset(mid[64:128, 8 + p, :], 0.0), writes=[R_mid[8 + p]])
                    tk.emit("pool", lambda p=p: nc.gpsimd.memset(mid[0:64, 12 + p, :], 0.0), writes=[R_mid[12 + p]])
                tt("dve", mid[0:64, 8 + p, :], R_mid[8 + p], t1[0:64, :], R_t1, t2[0:64, :], R_t2, ALU.add)
                tt("dve", mid[64:128, 12 + p, :], R_mid[12 + p], t1[64:128, :], R_t1, t2[64:128, :], R_t2, ALU.add)
            if j == 0:
                dump("qn", mid[:, 0:8, :], R_mid[0:8])
                dump("qr", mid[:, 8:12, :], R_mid[8:12])
                dump("krope", krope[:, 0:T], R_krope)

            nkc = 4 * (j + 1)
            n = T * (j + 1)
            LA = 2

            def kv_load(hh, lo=0, hi=None):
                hi = n if hi is None else hi
                sl = hh % 2
                blks = range(lo // T, (hi + T - 1) // T)
                dma("sp", kring[:, sl, lo:hi], R_kr[sl], Kc[hh, :, lo:hi], [R_Kcs[b_] for b_ in blks], ds_kl[sl])
                dma("sp", vring[:, sl, lo:hi], R_vr[sl], Vc[hh, :, lo:hi], [R_Vcs[b_] for b_ in blks], ds_vl[sl])

            seq = [(hh, kc) for hh in range(H) for kc in range(nkc)]
            pts = {}
            grp = {}
            den_started = {}
            den_q = []
            gcount = [0]
            acc32 = [t1, t2]
            R_acc32 = [R_t1, R_t2]

            def emit_S(i):
                hh, kc = seq[i]
                sl = hh % 2
                half = (hh // 4) * 64
                p = hh % 4
                c = kc - 4 * j
                q0 = 128 * c if c > 0 else 0
                sbk = next_mm_bank()
                mm(ps[sbk][:, q0:T], R_ps[sbk], kring[:, sl, kc * 128:(kc + 1) * 128], R_kr[sl],
                   mid[:, hh, q0:T], R_mid[hh], True, False)
                mm(ps[sbk][:, q0:T], R_ps[sbk], krope[:, kc * 128:(kc + 1) * 128], R_krope,
                   mid[:, 8 + hh, q0:T], R_mid[8 + hh], False, True)
                pi = 24 + (i % 4)
                pts[i] = (pi, q0)
                act(mid[:, pi, q0:T], R_mid[pi], ps[sbk][:, q0:T], R_ps[sbk], AF.Exp, scale=SCALE)
                if c >= 0:
                    if j == 0:
                        act(mid[64:128, pi, q0:q0 + 64], R_mid[pi], ps[sbk][64:128, q0:q0 + 64], R_ps[sbk],
                            AF.Copy, scale=0.0)
                    else:
                        tk.emit("pool", lambda pi=pi, q0=q0: nc.gpsimd.memset(mid[64:128, pi, q0:q0 + 64], 0.0),
                                writes=[R_mid[pi]])

            def emit_PV(i):
                hh, kc = seq[i]
                sl = hh % 2
                ob = 6 + (hh % 2)
                db = 4 + (hh % 2)
                pi, q0 = pts.pop(i)
                first = kc == 0
                last = kc == nkc - 1
                mm(ps[ob][:, q0:T], R_ps[ob], vring[:, sl, kc * 128:(kc + 1) * 128], R_vr[sl],
                   mid[:, pi, q0:T], R_mid[pi], first, last)

                def den_mm(rhs, r_rhs, q0_, last_, hh=hh, db=db):
                    st_ = not den_started.get(hh, False)
                    den_started[hh] = True
                    mm(ps[db][:, q0_:T], R_ps[db], ones[:, :], R_const, rhs, r_rhs, st_, last_)

                c = kc - 4 * j
                if c < 0:
                    pos = kc % 4
                    stg = grp.setdefault(hh, {})
                    if pos == 0:
                        stg["p0"] = pi
                    elif pos == 1:
                        g = gcount[0] % 2
                        stg["g"] = g
                        p0 = stg["p0"]
                        tt("dve", acc32[g][:, :], R_acc32[g], mid[:, p0, :], R_mid[p0], mid[:, pi, :], R_mid[pi], ALU.add)
                    elif pos == 2:
                        g = stg["g"]
                        tt("dve", acc32[g][:, :], R_acc32[g], acc32[g][:, :], R_acc32[g], mid[:, pi, :], R_mid[pi], ALU.add)
                    else:
                        g = stg["g"]
                        tt("dve", accb[:, g, :], R_accb[g], acc32[g][:, :], R_acc32[g], mid[:, pi, :], R_mid[pi], ALU.add)
                        gcount[0] += 1
                        den_q.append((i + 2, lambda g=g, den_mm=den_mm: den_mm(accb[:, g, :], R_accb[g], 0, False)))
                while den_q and (den_q[0][0] <= i or last):
                    den_q.pop(0)[1]()
                if c >= 0:
                    den_mm(mid[:, pi, q0:T], R_mid[pi], q0, last)
                if last:
                    act(rscr2[:, :], R_rscr2, ps[db][:, :], R_ps[db], AF.Ln)
                    act(rden[:, :], R_rden, rscr2[:, :], R_rscr2, AF.Exp, scale=-1.0)
                    tt("dve", mid[:, 16 + hh, :], R_mid[16 + hh], ps[ob][:, :], R_ps[ob], rden[:, :], R_rden, ALU.mult)
                    if hh + 2 < H:
                        kv_load(hh + 2)

            mark(f"t{j}:attn")
            kv_load(0, c0, n)
            kv_load(1, c0, n)
            for i in range(len(seq) + LA):
                if i < len(seq):
                    emit_S(i)
                if i - LA >= 0:
                    emit_PV(i - LA)
            if j == 0:
                dump("oT", mid[:, 16:24, :], R_mid[16:24])

            mark(f"t{j}:wo")
            for b in range(2):
                slot = wnext(("w", f"wo_{b}"))
                wv = wring[:, slot, :].rearrange("p (k m) -> p k m", k=8)
                for cc in range(4):
                    d = 4 * b + cc
                    bank = next_mm_bank()
                    for k in range(8):
                        mm(ps[bank][:, :], R_ps[bank], wv[:, k, cc * 128:(cc + 1) * 128], R_w[slot],
                           mid[:, 16 + k, :], R_mid[16 + k], k == 0, k == 7)
                    ts("dve", tA[:, d, :], R_tA[d], ps[bank][:, :], R_ps[bank], 1.0, None, ALU.mult)
                    act(sq[:, d, :], R_sq[d], ps[bank][:, :], R_ps[bank], AF.Square)
            mark(f"t{j}:post_mix1")
            postnorm("mix_post_g1", after=lambda c: hn_scale(c, 1))
            if j == 0:
                dump("h_mix1", h_t[:, :, :], R_h)
            mark(f"t{j}:ffn1")
            ffn(1, hook=(lambda: early_mixpre(j + 1)) if j + 1 < ntiles else None, pre_scaled=True)

            hb_, Rb_ = h_t, R_h
            stores.append(lambda hb_=hb_, Rb_=Rb_, c0=c0, j=j: dma(
                "sp", outT[:, c0:c0 + T].rearrange("(c p) t -> p c t", p=128), R_outs[j % 2],
                hb_[:, :, :], Rb_, ds_o[j % 2]))

        stopped = False
        try:
            for j in range(ntiles):
                tile_body(j)
        except _Stop:
            stopped = True
        run_pending()
        while stores:
            stores.pop(0)()
        assert stopped or st["next"] == len(sched), (st["next"], len(sched))
        toks = [R_outs[0].w, R_outs[1].w]
        for d_ in ds_dbg.values():
            if d_.val:
                toks.append((d_.key, d_.sem, d_.val, None))
        tk.wait_all("sp", toks)
        for en in ("pe", "act", "dve", "pool"):
            e = tk.engs[en]
            if e.count:
                tk.wait_all("sp", [(en, e.sem, e.count, en)])
        build_program.stats = (tk.ninst, tk.nwaits)
        build_program.marks = marks
    return nc


_CACHE = {}


def _prepare(inputs):
    inp = {k: np.asarray(v) for k, v in inputs.items()}
    wblocks, wnames = build_weight_blocks(inp)
    vecs = build_vecs(inp)
    cos2, sinS = rope_tables()
    return inp, wblocks, wnames, vecs, cos2, sinS


def kernel(**inputs):
    inp, wblocks, wnames, vecs, cos2, sinS = _prepare(inputs)
    nc = build_program(wblocks.shape[0], wnames)
    x = inp["x"]
    in_maps = []
    for b in range(NCORES):
        in_maps.append({
            "xT": np.ascontiguousarray(x[b].T),
            "w32": wblocks,
            "vecs": vecs,
            "cos2": cos2,
            "sinS": sinS,
        })
    res = run_bass_kernel_spmd(nc, in_maps, core_ids=list(range(NCORES)))
    out = np.stack([np.ascontiguousarray(np.asarray(r["outT"]).T) for r in res.results], 0)
    return out.astype(np.float32)
```

```python
import math
from contextlib import ExitStack

import numpy as np
import concourse.bass as bass
import concourse.mybir as mybir
from concourse.bass_utils import run_bass_kernel_spmd

F32 = mybir.dt.float32
BF16 = mybir.dt.bfloat16
ALU = mybir.AluOpType
AF = mybir.ActivationFunctionType

D = 1024
S = 4096
T = 512
NT = S // T
DFF = 4096
H = 8
CW = 31
HALO = CW - 1
QL = 384
KVL = 256
RMS_EPS = 1e-6
LN_EPS = 1e-5
SCALE = 192 ** -0.5
NCORES = 8
WBLK = 4096


def _proj_block(W, cols, kc):
    sub = W[:, cols].reshape(kc, 128, len(cols))
    return np.ascontiguousarray(sub.transpose(1, 0, 2)).reshape(128, kc * len(cols))


def _pad_block(b):
    out = np.zeros((128, WBLK), np.float32)
    out[:, : b.shape[1]] = b
    return out


def build_weight_blocks(inp):
    blocks = []
    names = {}

    def add(name, b):
        names[name] = len(blocks)
        blocks.append(_pad_block(b))

    ar = np.arange
    w1 = inp["conv_w_pw1"][0]
    for b in range(4):
        cols = np.concatenate([
            ar(128) + (2 * b) * 128, ar(128) + 1024 + (2 * b) * 128,
            ar(128) + (2 * b + 1) * 128, ar(128) + 1024 + (2 * b + 1) * 128])
        add(f"pw1_{b}", _proj_block(w1, cols, 8))
    w2 = inp["conv_w_pw2"][0]
    for b in range(2):
        add(f"pw2_{b}", _proj_block(w2, ar(512) + 512 * b, 8))
    def add_ffn(l):
        f1 = inp["w_ff1"][l]
        f2 = inp["w_ff2"][l]
        for b in range(8):
            add(f"ff1_{l}_{b}", _proj_block(f1, ar(512) + 512 * b, 8))
        for d in range(8):
            add(f"ff2_{l}_{d}", _proj_block(f2, ar(128) + 128 * d, 32))

    add_ffn(0)
    kr = inp["kv_w_kr"]
    sw = (ar(64) + 32) % 64
    kva = np.concatenate([inp["kv_w_dkv"], kr, kr, kr[:, sw], kr[:, sw]], axis=1)
    add("kva", _proj_block(kva, ar(512), 8))
    add("dq", _proj_block(inp["mla_w_dq"][0], ar(384), 8))
    kvb = np.concatenate([inp["kv_w_uk"], inp["kv_w_uv"]], axis=1)
    add("kvb", _proj_block(kvb, ar(2048), 2))
    uq = inp["mla_w_uq"][0]
    cols = []
    for h in range(8):
        cols.append(h * 192 + ar(128))
    for p in range(4):
        cols.append(p * 192 + 128 + ar(64))
        cols.append((p + 4) * 192 + 128 + ar(64))
    for p in range(4):
        cols.append(p * 192 + 128 + sw)
        cols.append((p + 4) * 192 + 128 + sw)
    cols = np.concatenate(cols)
    add("uq_0", _proj_block(uq, cols[:1024], 3))
    add("uq_1", _proj_block(uq, cols[1024:], 3))
    wo = inp["mla_w_o"][0]
    for b in range(2):
        add(f"wo_{b}", _proj_block(wo, ar(512) + 512 * b, 8))
    add_ffn(1)
    return np.stack(blocks, 0), names


VEC_SPECS = [("mix_pre_g0", 8), ("mix_pre_g1", 8), ("mix_post_g0", 8), ("mix_post_g1", 8),
             ("ffn_pre_g0", 8), ("ffn_pre_g1", 8), ("ffn_post_g0", 8), ("ffn_post_g1", 8),
             ("b_pw1", 16), ("w_dw", 8 * CW), ("b_dw", 8), ("ln_g", 8), ("ln_b", 8),
             ("b_pw2", 8), ("q_norm_g", 3), ("kv_in_g", 8), ("kv_norm_g", 2)]
VEC_OFF = {}
_o = 0
for _n, _c in VEC_SPECS:
    VEC_OFF[_n] = _o
    _o += _c
NVEC = _o


def build_vecs(inp):
    def colz(v):
        return np.ascontiguousarray(v.reshape(-1, 128).T)
    parts = {}
    for nm in ("mix_pre_g", "mix_post_g", "ffn_pre_g", "ffn_post_g"):
        for l in range(2):
            parts[f"{nm}{l}"] = colz(inp[nm][l])
    parts["b_pw1"] = colz(inp["conv_b_pw1"][0])
    wd = inp["conv_w_dw"][0]
    parts["w_dw"] = np.ascontiguousarray(
        wd.reshape(CW, 8, 128).transpose(2, 1, 0)).reshape(128, 8 * CW)
    parts["b_dw"] = colz(inp["conv_b_dw"][0])
    parts["ln_g"] = colz(inp["conv_ln_g"][0])
    parts["ln_b"] = colz(inp["conv_ln_b"][0])
    parts["b_pw2"] = colz(inp["conv_b_pw2"][0])
    parts["q_norm_g"] = colz(inp["mla_q_norm_g"][0])
    parts["kv_in_g"] = colz(inp["kv_in_g"])
    parts["kv_norm_g"] = colz(inp["kv_norm_g"])
    return np.ascontiguousarray(
        np.concatenate([parts[n] for n, _ in VEC_SPECS], axis=1).astype(np.float32))


def rope_tables():
    inv = (1.0 / (np.float32(10000.0) ** (np.arange(0, 64, 2, dtype=np.float32) / np.float32(64)))).astype(np.float32)
    pos = np.arange(S, dtype=np.float32)
    ang = (pos[:, None] * inv[None, :]).astype(np.float32)
    cos = np.cos(ang).astype(np.float32).T
    sin = np.sin(ang).astype(np.float32).T
    cos2 = np.concatenate([cos, cos, cos, cos], 0)
    sinS = np.concatenate([-sin, sin, -sin, sin], 0)
    return np.ascontiguousarray(cos2), np.ascontiguousarray(sinS)


class Res:
    __slots__ = ("name", "w", "r", "excl")

    def __init__(self, name, excl=False):
        self.name = name
        self.w = None
        self.r = {}
        self.excl = excl


class DSem:
    __slots__ = ("key", "sem", "val")

    def __init__(self, key, sem):
        self.key = key
        self.sem = sem
        self.val = 0


class _Eng:
    def __init__(self, name, h, sem):
        self.name = name
        self.h = h
        self.sem = sem
        self.count = 0
        self.seen = {}


class TK:
    def __init__(self):
        self.engs = {}
        self.nwaits = 0
        self.ninst = 0

    def add_engine(self, name, h, sem):
        self.engs[name] = _Eng(name, h, sem)

    def emit(self, eng, fn, reads=(), writes=(), signal=True, dsem=None):
        e = self.engs[eng]
        deps = {}

        def add(tok, same_ok):
            if tok is None:
                return
            key, sem, val, src = tok
            if src == eng and not same_ok:
                return
            cur = deps.get(key)
            if cur is None or cur[1] < val:
                deps[key] = (sem, val, src)

        for r in reads:
            add(r.w, True)
            if r.excl:
                for t in r.r.values():
                    add(t, False)
        for w in writes:
            add(w.w, False)
            for t in w.r.values():
                add(t, False)
        for key, (sem, val, src) in deps.items():
            if e.seen.get(key, 0) >= val:
                continue
            if src is not None:
                assert self.engs[src].count >= val, (eng, src, val, self.engs[src].count)
            e.h.wait_ge(sem, val)
            e.seen[key] = val
            self.nwaits += 1
        ins = fn()
        self.ninst += 1
        if dsem is not None:
            dsem.val += 16
            ins.then_inc(dsem.sem, 16)
            tok = (dsem.key, dsem.sem, dsem.val, None)
        elif signal:
            e.count += 1
            ins.then_inc(e.sem, 1)
            tok = (eng, e.sem, e.count, eng)
        else:
            tok = (eng, e.sem, e.count + 1, eng)
        for w in writes:
            w.w = tok
            w.r = {}
        for r in reads:
            cur = r.r.get(tok[0])
            if cur is None or cur[2] < tok[2]:
                r.r[tok[0]] = tok
        return tok

    def wait_all(self, eng, toks):
        e = self.engs[eng]
        for tok in toks:
            if tok is None:
                continue
            key, sem, val, src = tok
            if e.seen.get(key, 0) >= val:
                continue
            e.h.wait_ge(sem, val)
            e.seen[key] = val


class _Stop(Exception):
    pass


def build_program(nblk, wnames, ntiles=NT, dumps=(), skip_cast=False, stop_after=None):
    nc = bass.Bass("TRN2", target_bir_lowering=False)
    xT = nc.dram_tensor("xT", [D, S], F32, kind="ExternalInput").ap()
    w32 = nc.dram_tensor("w32", [nblk, 128, WBLK], F32, kind="ExternalInput").ap()
    vecs_d = nc.dram_tensor("vecs", [128, NVEC], F32, kind="ExternalInput").ap()
    cos_d = nc.dram_tensor("cos2", [128, S], F32, kind="ExternalInput").ap()
    sin_d = nc.dram_tensor("sinS", [128, S], F32, kind="ExternalInput").ap()
    outT = nc.dram_tensor("outT", [D, S], F32, kind="ExternalOutput").ap()
    wbf = nc.dram_tensor("wbf", [nblk, 128, WBLK], BF16, kind="Internal").ap()
    Kc = nc.dram_tensor("Kc", [H, 128, S], BF16, kind="Internal").ap()
    diagbf = nc.dram_tensor("diagbf", [8, 128, WBLK], BF16, kind="Internal").ap()
    Vc = nc.dram_tensor("Vc", [H, 128, S], BF16, kind="Internal").ap()
    dump_aps = {}
    for nm, shape, dt in dumps:
        dump_aps[nm] = nc.dram_tensor("dbg_" + nm, list(shape), dt, kind="ExternalOutput").ap()

    es = ExitStack()
    with es:
        def sb(name, shape, dt):
            return es.enter_context(nc.sbuf_tensor(name, list(shape), dt))

        def sem(name):
            return es.enter_context(nc.semaphore(name))

        tk = TK()
        tk.add_engine("pe", nc.tensor, sem("s_pe"))
        tk.add_engine("act", nc.scalar, sem("s_act"))
        tk.add_engine("dve", nc.vector, sem("s_dve"))
        tk.add_engine("pool", nc.gpsimd, sem("s_pool"))
        tk.add_engine("sp", nc.sync, sem("s_sp"))

        h_a = sb("h", [128, 8, T], F32)
        h_b = sb("h2", [128, 8, T], F32)
        h_t = h_a
        tA = sb("tA", [128, 8, T], F32)
        hn = sb("hn", [128, 8, T], BF16)
        sq = sb("sq", [128, 8, T], BF16)
        mid = sb("mid", [128, 32, T], BF16)
        ubuf = sb("ubuf", [128, 8, HALO + T], BF16)
        sig = sb("sig", [128, 2, T], F32)
        wring = sb("wring", [128, 3, WBLK], BF16)
        rstd = sb("rstd", [128, T], F32)
        mean = sb("mean", [128, T], F32)
        t1 = sb("t1", [128, T], F32)
        t2 = sb("t2", [128, T], F32)
        rden = sb("rden", [128, T], F32)
        rstd2 = sb("rstd2", [128, T], F32)
        accb = sb("accb", [128, 2, T], BF16)
        rscr = sb("rscr", [128, T], F32)
        rscr2 = sb("rscr2", [128, T], F32)
        ckv = sb("ckv", [128, 2, T], BF16)
        cq = sb("cq", [128, 3, T], BF16)
        krope = sb("krope", [128, S], BF16)
        cos_t = sb("cos_t", [128, T], F32)
        sin_t = sb("sin_t", [128, T], F32)
        kring = sb("kring", [128, 2, S], BF16)
        vring = sb("vring", [128, 2, S], BF16)
        vecs = sb("vecs_sb", [128, NVEC], F32)
        ones = sb("ones", [128, 128], BF16)
        ident = sb("ident", [128, 128], BF16)
        epsc = sb("epsc", [128, 2], F32)
        identf = sb("identf", [128, 128], F32)

        ps = [es.enter_context(nc.psum_tensor(f"ps{i}", [128, T], F32)) for i in range(8)]

        R_ha = [Res(f"h{c}") for c in range(8)]
        R_hb = [Res(f"hb{c}") for c in range(8)]
        R_h = R_ha
        hbufs = [(h_a, R_ha), (h_b, R_hb)]
        R_tA = [Res(f"tA{c}") for c in range(8)]
        R_hn = [Res(f"hn{c}") for c in range(8)]
        R_sq = [Res(f"sq{c}") for c in range(8)]
        R_mid = [Res(f"mid{c}") for c in range(32)]
        R_ub = [Res(f"ub{c}") for c in range(8)]
        R_sig = [Res("sig0"), Res("sig1")]
        R_w = [Res(f"w{i}") for i in range(3)]
        R_rstd, R_mean, R_t1, R_t2, R_rden = Res("rstd"), Res("mean"), Res("t1"), Res("t2"), Res("rden")
        R_rscr, R_rscr2 = Res("rscr"), Res("rscr2")
        R_rstd2 = Res("rstd2")
        R_accb = [Res("accb0"), Res("accb1")]

        R_ckv = [Res("ckv0"), Res("ckv1")]
        R_cq = [Res(f"cq{i}") for i in range(3)]
        R_krope = Res("krope")
        R_cos, R_sin = Res("cos"), Res("sin")
        R_kr = [Res("kr0"), Res("kr1")]
        R_vr = [Res("vr0"), Res("vr1")]
        R_const = Res("const")
        R_ps = [Res(f"ps{i}", excl=True) for i in range(8)]
        R_wbf = [Res(f"wbf{i}") for i in range(nblk)]
        R_Kcs = [Res(f"Kc{i}") for i in range(NT)]
        R_Vcs = [Res(f"Vc{i}") for i in range(NT)]
        R_outs = [Res("out0"), Res("out1")]

        ds_w = [DSem(f"dw{i}", sem(f"d_w{i}")) for i in range(3)]
        ds_x = [DSem("dx0", sem("d_x0")), DSem("dx1", sem("d_x1"))]
        ds_o = [DSem("do0", sem("d_o0")), DSem("do1", sem("d_o1"))]
        ds_c = DSem("dc", sem("d_c"))
        ds_cos = DSem("dcos", sem("d_cos"))
        ds_sin = DSem("dsin", sem("d_sin"))
        ds_ks = DSem("dks", sem("d_ks"))
        ds_vs = DSem("dvs", sem("d_vs"))
        ds_kl = [DSem(f"dkl{i}", sem(f"d_kl{i}")) for i in range(2)]
        ds_vl = [DSem(f"dvl{i}", sem(f"d_vl{i}")) for i in range(2)]
        ds_pro = [DSem(f"dpro{i}", sem(f"d_pro{i}")) for i in range(4)]
        ds_dbg = {nm: DSem("ddbg_" + nm, sem("d_dbg_" + nm)) for nm, _, _ in dumps}

        def V(name, i=0):
            o = VEC_OFF[name] + i
            return vecs[:, o:o + 1]

        marks = []
        mmc = [0]

        def mark(label):
            marks.append((mmc[0], label))

        def mm(out, r_out, lhsT, r_l, rhs, r_r, start, stop, signal=None):
            mmc[0] += 1
            if signal is None:
                signal = stop
            rd = [r_l, r_r] if isinstance(r_r, Res) else [r_l] + list(r_r)
            return tk.emit("pe", lambda: nc.tensor.matmul(out, lhsT, rhs, start=start, stop=stop),
                           reads=rd, writes=[r_out], signal=signal)

        def act(out, r_out, in_, r_in, func, bias=None, scale=None, extra_reads=()):
            kw = {}
            if bias is not None:
                kw["bias"] = bias
            if scale is not None:
                kw["scale"] = scale
            rins = r_in if isinstance(r_in, (list, tuple)) else [r_in]
            return tk.emit("act", lambda: nc.scalar.activation(out, in_, func, **kw),
                           reads=list(rins) + [R_const] + list(extra_reads), writes=[r_out])

        def ts(eng, out, r_out, in0, r_in, s1, s2, op0, op1=None):
            h = nc.vector if eng == "dve" else nc.gpsimd
            if op1 is None:
                f = lambda: h.tensor_scalar(out, in0, s1, None, op0)
            else:
                f = lambda: h.tensor_scalar(out, in0, s1, s2, op0, op1)
            return tk.emit(eng, f, reads=[r_in, R_const], writes=[r_out])

        def tt(eng, out, r_out, in0, r0, in1, r1, op):
            h = nc.vector if eng == "dve" else nc.gpsimd
            return tk.emit(eng, lambda: h.tensor_tensor(out, in0, in1, op),
                           reads=[r0, r1], writes=[r_out])

        def stt(out, r_out, in0, r0, scalar, in1, r1, op0, op1):
            return tk.emit("dve", lambda: nc.vector.scalar_tensor_tensor(out, in0, scalar, in1, op0, op1),
                           reads=[r0, r1, R_const], writes=[r_out])

        def dma(eng, out, r_out, in_, r_in, dsem, **kw):
            h = nc.sync if eng == "sp" else nc.gpsimd
            rd = [r_in] if isinstance(r_in, Res) else list(r_in)
            wr = [r_out] if isinstance(r_out, Res) else list(r_out)
            return tk.emit(eng, lambda: h.dma_start(out=out, in_=in_, **kw), reads=rd, writes=wr, dsem=dsem)

        def eps_ap(eps):
            return epsc[:, 0:1] if eps == RMS_EPS else epsc[:, 1:2]

        tk.emit("pool", lambda: nc.gpsimd.memset(epsc[:, 0:1], RMS_EPS), writes=[R_const])
        tk.emit("pool", lambda: nc.gpsimd.memset(epsc[:, 1:2], LN_EPS), writes=[R_const])
        dma("sp", vecs[:, :], R_const, vecs_d[:, :], Res("vecs_d"), ds_c)
        tk.emit("pool", lambda: nc.gpsimd.memset(ones[:, :], 1.0), writes=[R_const])
        tk.emit("pool", lambda: nc.gpsimd.memset(identf[:, :], 0.0), writes=[R_const])
        tk.emit("pool", lambda: nc.gpsimd.affine_select(
            identf[:, :], identf[:, :], [[-1, 128]], ALU.not_equal, 1.0, base=0, channel_multiplier=1),
            reads=[R_const], writes=[R_const])
        tk.emit("pool", lambda: nc.gpsimd.tensor_copy(ident[:, :], identf[:, :]),
                reads=[R_const], writes=[R_const])
        tk.emit("pool", lambda: nc.gpsimd.memset(ubuf[:, :, :], 0.0), writes=R_ub)

        R_diag = [Res(f"diag{c}") for c in range(8)]
        ds_diag = [DSem(f"ddg{c}", sem(f"d_dg{c}")) for c in range(8)]

        def gen_diags():
            for c in range(8):
                q4 = c % 4
                stage = mid[:, 8 * q4:8 * q4 + 8, :].rearrange("p a t -> p (a t)")
                dst = stage[:, 0:CW * 128].rearrange("p (k m) -> p k m", k=CW)
                o = VEC_OFF["w_dw"] + c * CW
                rr = R_mid[8 * q4:8 * q4 + 8]
                tk.emit("pool", lambda dst=dst, o=o: nc.gpsimd.tensor_tensor(
                    dst, ident[:, :].unsqueeze(1).broadcast_to([128, CW, 128]),
                    vecs[:, o:o + CW].unsqueeze(2).broadcast_to([128, CW, 128]), ALU.mult),
                    reads=[R_const], writes=rr)
                tk.emit("pool", lambda stage=stage: nc.gpsimd.memset(stage[:, CW * 128:WBLK], 0.0), writes=rr)
                dma("pool", diagbf[c, :, :], R_diag[c], stage[:, :], rr, ds_diag[c])

        if not skip_cast:
            lanes = [Res(f"pro_lane{i}") for i in range(4)]

            def cast_blk(b):
                tk.emit("pool", lambda b=b: nc.gpsimd.dma_start(
                    out=wbf[b], in_=w32[b], max_dma_last_dim=2048 * 4),
                    reads=[], writes=[R_wbf[b], lanes[b % 4]], dsem=ds_pro[b % 4])

            for b in range(min(4, nblk)):
                cast_blk(b)
            gen_diags()
            for b in range(4, nblk):
                cast_blk(b)
        else:
            gen_diags()

        sched = []
        for j in range(ntiles):
            sched += [("w", "pw1_0"), ("w", "pw1_1"), ("diag", 0), ("diag", 1), ("w", "pw1_2"),
                      ("diag", 2), ("diag", 3), ("w", "pw1_3"), ("diag", 4), ("diag", 5),
                      ("diag", 6), ("diag", 7), ("w", "pw2_0"), ("w", "pw2_1")]
            sched += [("w", f"ff1_0_{b}") for b in range(8)]
            sched += [("w", f"ff2_0_{b}") for b in range(8)]
            sched += [("w", "kva"), ("w", "dq"), ("w", "kvb"), ("w", "uq_0"), ("w", "uq_1"),
                      ("w", "wo_0"), ("w", "wo_1")]
            sched += [("w", f"ff1_1_{b}") for b in range(8)]
            sched += [("w", f"ff2_1_{b}") for b in range(8)]
        st = {"issued": 0, "next": 0}

        def issue_one():
            i = st["issued"]
            if i >= len(sched):
                return
            slot = i % 3
            kind, arg = sched[i]
            if kind == "w":
                b = wnames[arg]
                dma("sp", wring[:, slot, :], R_w[slot], wbf[b], R_wbf[b], ds_w[slot])
            else:
                c = arg
                dma("sp", wring[:, slot, :], R_w[slot], diagbf[c], R_diag[c], ds_w[slot])
            st["issued"] += 1

        def wnext(expect):
            i = st["next"]
            assert sched[i] == expect, (sched[i], expect)
            while st["issued"] < min(len(sched), i + 3):
                issue_one()
            st["next"] += 1
            return i % 3

        pscur = {"mm": 0}

        def next_mm_bank():
            b = pscur["mm"]
            pscur["mm"] = (b + 1) % 4
            return b

        def stats(srcs, bank):
            n = len(srcs)
            for i, (ap, r) in enumerate(srcs):
                mm(ps[bank][:, :], R_ps[bank], ones[:, :], R_const, ap, r, i == 0, i == n - 1)

        def rstd_from(bank, dim, eps, out=None, r_out=None, scr=None, r_scr=None):
            out = rstd if out is None else out
            r_out = R_rstd if r_out is None else r_out
            scr = rscr if scr is None else scr
            r_scr = R_rscr if r_scr is None else r_scr
            act(scr[:, :], r_scr, ps[bank][:, :], R_ps[bank], AF.Ln, bias=eps_ap(eps), scale=1.0 / dim)
            act(out[:, :], r_out, scr[:, :], r_scr, AF.Exp, scale=-0.5)
            return

        def _rstd_from_old(bank, dim, eps):
            act(rscr[:, :], R_rscr, ps[bank][:, :], R_ps[bank], AF.Ln, bias=eps_ap(eps), scale=1.0 / dim)
            act(rstd[:, :], R_rstd, rscr[:, :], R_rscr, AF.Exp, scale=-0.5)

        def rms_stats(src, r_src, nch, dim):
            dump("ck_load", None, None)
            for c in range(nch):
                act(sq[:, c, :], R_sq[c], src[:, c, :], r_src[c], AF.Square)
            dump("ck_sq", None, None)
            stats([(sq[:, c, :], R_sq[c]) for c in range(nch)], 4)
            dump("ck_stats", None, None)
            rstd_from(4, dim, RMS_EPS)
            dump("ck_rstd", None, None)

        def prenorm(gname, src=None, r_src=None, dst=None, r_dst=None, nch=8, rs=None, r_rs=None):
            src = h_t if src is None else src
            r_src = R_h if r_src is None else r_src
            dst = hn if dst is None else dst
            r_dst = R_hn if r_dst is None else r_dst
            rs = rstd if rs is None else rs
            r_rs = R_rstd if r_rs is None else r_rs
            for c in range(nch):
                stt(dst[:, c, :], r_dst[c], src[:, c, :], r_src[c], V(gname, c), rs[:, :], r_rs,
                    ALU.mult, ALU.mult)

        def early_mixpre(jn):
            hb, Rb = hbufs[jn % 2]
            for c in range(8):
                act(hn[:, c, :], R_hn[c], hb[:, c, :], Rb[c], AF.Square)
            stats([(hn[:, c, :], R_hn[c]) for c in range(8)], 5)
            rstd_from(5, D, RMS_EPS, out=rstd2, r_out=R_rstd2, scr=rscr2, r_scr=R_rscr2)
            prenorm("mix_pre_g0", src=hb, r_src=Rb, rs=rstd2, r_rs=R_rstd2)

        stores = []
        pending = []

        def run_pending(n=None):
            k = len(pending) if n is None else min(n, len(pending))
            for _ in range(k):
                pending.pop(0)()

        def postnorm(gname, defer=False, after=None):
            stats([(sq[:, c, :], R_sq[c]) for c in range(8)], 4)
            rstd_from(4, D, RMS_EPS)
            hb, Rb = h_t, R_h

            def apply(c):
                stt(tA[:, c, :], R_tA[c], tA[:, c, :], R_tA[c], V(gname, c), rstd[:, :], R_rstd,
                    ALU.mult, ALU.mult)
                tt("dve", hb[:, c, :], Rb[c], hb[:, c, :], Rb[c], tA[:, c, :], R_tA[c], ALU.add)
                if after is not None:
                    after(c)

            for c in range(8):
                if defer:
                    pending.append(lambda c=c: apply(c))
                else:
                    apply(c)

        def proj8(blockname, nout, evac, kouter=False, mid_hook=None):
            slot = wnext(("w", blockname))
            wv = wring[:, slot, :].rearrange("p (k m) -> p k m", k=8)
            if kouter:
                banks = [next_mm_bank() for _ in range(nout)]
                for k in range(8):
                    for cc in range(nout):
                        mm(ps[banks[cc]][:, :], R_ps[banks[cc]], wv[:, k, cc * 128:(cc + 1) * 128], R_w[slot],
                           hn[:, k, :], R_hn[k], k == 0, k == 7)
                if mid_hook is not None:
                    mid_hook()
                for cc in range(nout):
                    evac(cc, banks[cc])
                return
            for cc in range(nout):
                bank = next_mm_bank()
                for k in range(8):
                    mm(ps[bank][:, :], R_ps[bank], wv[:, k, cc * 128:(cc + 1) * 128], R_w[slot],
                       hn[:, k, :], R_hn[k], k == 0, k == 7)
                evac(cc, bank)

        def hn_scale(c, l):
            ts("dve", hn[:, c, :], R_hn[c], h_t[:, c, :], R_h[c], V(f"ffn_pre_g{l}", c), None, ALU.mult)

        def ffn(l, hook=None, pre_scaled=False):
            mark(f"ffn{l}:prenorm")
            if not pre_scaled:
                for c in range(8):
                    hn_scale(c, l)
            for c in range(8):
                act(sq[:, c, :], R_sq[c], h_t[:, c, :], R_h[c], AF.Square)

            def stats_hook():
                stats([(sq[:, c, :], R_sq[c]) for c in range(8)], 4)
                rstd_from(4, D, RMS_EPS)

            for b in range(8):
                def ev(cc, bank, b=b):
                    m = 4 * b + cc
                    stt(mid[:, m, :], R_mid[m], ps[bank][:, :], R_ps[bank], 0.0, rstd[:, :], R_rstd,
                        ALU.max, ALU.mult)
                    tt("dve", mid[:, m, :], R_mid[m], mid[:, m, :], R_mid[m], mid[:, m, :], R_mid[m], ALU.mult)
                proj8(f"ff1_{l}_{b}", 4, ev, kouter=(b == 0), mid_hook=stats_hook if b == 0 else None)
            mark(f"ffn{l}:ff2")
            for d in range(8):
                slot = wnext(("w", f"ff2_{l}_{d}"))
                wv = wring[:, slot, :].rearrange("p (k m) -> p k m", k=32)
                bank = next_mm_bank()
                for k in range(32):
                    mm(ps[bank][:, :], R_ps[bank], wv[:, k, :], R_w[slot], mid[:, k, :], R_mid[k],
                       k == 0, k == 31)
                ts("dve", tA[:, d, :], R_tA[d], ps[bank][:, :], R_ps[bank], 1.0, None, ALU.mult)
                act(sq[:, d, :], R_sq[d], ps[bank][:, :], R_ps[bank], AF.Square)
                if d == 1 and hook is not None:
                    hook()
            mark(f"ffn{l}:post")
            postnorm(f"ffn_post_g{l}", defer=(l == 1))

        def dump(nm, ap, rs):
            if nm in dump_aps:
                dma("sp", dump_aps[nm], Res("dbg_" + nm), ap, rs, ds_dbg[nm])
            if stop_after == nm:
                raise _Stop()

        def tile_body(j):
            nonlocal h_t, R_h
            c0 = j * T
            h_t, R_h = hbufs[j % 2]
            if j == 0:
                dma("sp", h_t[:, :, :], R_h, xT[:, 0:T].rearrange("(c p) t -> p c t", p=128),
                    Res("x_d"), ds_x[0])
            dma("sp", cos_t[:, :], R_cos, cos_d[:, c0:c0 + T], Res("cos_d"), ds_cos)
            dma("sp", sin_t[:, :], R_sin, sin_d[:, c0:c0 + T], Res("sin_d"), ds_sin)

            mark(f"t{j}:mixpre0")
            if j == 0:
                rms_stats(h_t, R_h, 8, D)
                prenorm("mix_pre_g0")
            if j == 0:
                dump("hn0", hn[:, :, :], R_hn)

            def glu_block(b):
                slot = wnext(("w", f"pw1_{b}"))
                wv = wring[:, slot, :].rearrange("p (k m) -> p k m", k=8)
                if b == 0:
                    gb = [next_mm_bank() for _ in range(4)]
                    for k in range(8):
                        for g4 in range(4):
                            mm(ps[gb[g4]][:, :], R_ps[gb[g4]], wv[:, k, g4 * 128:(g4 + 1) * 128], R_w[slot],
                               hn[:, k, :], R_hn[k], k == 0, k == 7)
                for i in range(2):
                    c = 2 * b + i
                    if b == 0:
                        ba, bg = gb[2 * i], gb[2 * i + 1]
                    else:
                        ba = next_mm_bank()
                        for k in range(8):
                            mm(ps[ba][:, :], R_ps[ba], wv[:, k, (2 * i) * 128:(2 * i + 1) * 128], R_w[slot],
                               hn[:, k, :], R_hn[k], k == 0, k == 7)
                        bg = next_mm_bank()
                        for k in range(8):
                            mm(ps[bg][:, :], R_ps[bg], wv[:, k, (2 * i + 1) * 128:(2 * i + 2) * 128], R_w[slot],
                               hn[:, k, :], R_hn[k], k == 0, k == 7)
                    si = c % 2
                    act(sig[:, si, :], R_sig[si], ps[bg][:, :], R_ps[bg], AF.Sigmoid, bias=V("b_pw1", 8 + c))
                    stt(ubuf[:, c, HALO:HALO + T], R_ub[c], ps[ba][:, :], R_ps[ba], V("b_pw1", c),
                        sig[:, si, :], R_sig[si], ALU.add, ALU.mult)
                    run_pending(1)

            def conv_chunk(c):
                slot = wnext(("diag", c))
                wv = wring[:, slot, 0:CW * 128].rearrange("p (k m) -> p k m", k=CW)
                bank = next_mm_bank()
                for k in range(CW):
                    mm(ps[bank][:, :], R_ps[bank], wv[:, k, :], R_w[slot], ubuf[:, c, k:k + T], R_ub[c],
                       k == 0, k == CW - 1)
                act(tA[:, c, :], R_tA[c], ps[bank][:, :], R_ps[bank], AF.Identity, bias=V("b_dw", c))
                act(mid[:, c, :], R_mid[c], ps[bank][:, :], R_ps[bank], AF.Identity, bias=V("b_dw", c))
                act(sq[:, c, :], R_sq[c], ps[bank][:, :], R_ps[bank], AF.Square, bias=V("b_dw", c))
                tk.emit("pool", lambda: nc.gpsimd.tensor_copy(ubuf[:, c, 0:HALO], ubuf[:, c, T:T + HALO]),
                        reads=[R_ub[c]], writes=[R_ub[c]])

            mark(f"t{j}:pw1conv")
            glu_block(0)
            glu_block(1)
            conv_chunk(0)
            conv_chunk(1)
            glu_block(2)
            conv_chunk(2)
            conv_chunk(3)
            glu_block(3)
            run_pending()
            while stores:
                stores.pop(0)()
            for c in range(4, 8):
                conv_chunk(c)
            if j == 0:
                dump("conv0", tA[:, :, :], R_tA)

            mark(f"t{j}:LN")
            stats([(mid[:, c, :], R_mid[c]) for c in range(8)], 4)
            stats([(sq[:, c, :], R_sq[c]) for c in range(8)], 5)
            ts("dve", mean[:, :], R_mean, ps[4][:, :], R_ps[4], 1.0 / D, None, ALU.mult)
            tt("dve", t1[:, :], R_t1, mean[:, :], R_mean, mean[:, :], R_mean, ALU.mult)
            stt(t2[:, :], R_t2, ps[5][:, :], R_ps[5], 1.0 / D, t1[:, :], R_t1, ALU.mult, ALU.subtract)
            act(rscr[:, :], R_rscr, t2[:, :], R_t2, AF.Ln, bias=eps_ap(LN_EPS), scale=1.0)
            act(rstd[:, :], R_rstd, rscr[:, :], R_rscr, AF.Exp, scale=-0.5)
            for c in range(8):
                tt("dve", tA[:, c, :], R_tA[c], tA[:, c, :], R_tA[c], mean[:, :], R_mean, ALU.subtract)
                tt("dve", tA[:, c, :], R_tA[c], tA[:, c, :], R_tA[c], rstd[:, :], R_rstd, ALU.mult)
                act(hn[:, c, :], R_hn[c], tA[:, c, :], R_tA[c], AF.Silu, bias=V("ln_b", c), scale=V("ln_g", c))
            if j == 0:
                dump("lnsilu0", hn[:, :, :], R_hn)

            for b in range(2):
                def ev(cc, bank, b=b):
                    d = 4 * b + cc
                    act(tA[:, d, :], R_tA[d], ps[bank][:, :], R_ps[bank], AF.Identity, bias=V("b_pw2", d))
                    act(sq[:, d, :], R_sq[d], ps[bank][:, :], R_ps[bank], AF.Square, bias=V("b_pw2", d))
                proj8(f"pw2_{b}", 4, ev, kouter=(b == 0))
            mark(f"t{j}:post_mix0")
            postnorm("mix_post_g0", after=lambda c: hn_scale(c, 0))
            if j == 0:
                dump("h_mix0", h_t[:, :, :], R_h)
            if j + 1 < ntiles:
                hn_, Rn_ = hbufs[(j + 1) % 2]
                dma("sp", hn_[:, :, :], Rn_, xT[:, c0 + T:c0 + 2 * T].rearrange("(c p) t -> p c t", p=128),
                    Res("x_d"), ds_x[(j + 1) % 2])
            mark(f"t{j}:ffn0")
            ffn(0, pre_scaled=True)
            if j == 0:
                dump("h_l0", h_t[:, :, :], R_h)

            mark(f"t{j}:kvnorm")
            if j > 0:
                for hh_ in (0, 1):
                    sl_ = hh_ % 2
                    dma("sp", kring[:, sl_, 0:c0], R_kr[sl_], Kc[hh_, :, 0:c0], R_Kcs[0:j], ds_kl[sl_])
                    dma("sp", vring[:, sl_, 0:c0], R_vr[sl_], Vc[hh_, :, 0:c0], R_Vcs[0:j], ds_vl[sl_])
            rms_stats(h_t, R_h, 8, D)
            prenorm("kv_in_g")
            prenorm("mix_pre_g1", dst=mid, r_dst=R_mid)
            slot = wnext(("w", "kva"))
            wv = wring[:, slot, :].rearrange("p (k m) -> p k m", k=8)
            kb = [next_mm_bank() for _ in range(4)]
            for k in range(8):
                for g4 in range(4):
                    mm(ps[kb[g4]][:, :], R_ps[kb[g4]], wv[:, k, g4 * 128:(g4 + 1) * 128], R_w[slot],
                       hn[:, k, :], R_hn[k], k == 0, k == 7)
            for cc in range(2):
                ts("dve", tA[:, cc, :], R_tA[cc], ps[kb[cc]][:, :], R_ps[kb[cc]], 1.0, None, ALU.mult)
                act(sq[:, cc, :], R_sq[cc], ps[kb[cc]][:, :], R_ps[kb[cc]], AF.Square)
            bka, bkb = kb[2], kb[3]
            tt("dve", t1[:, :], R_t1, ps[bka][:, :], R_ps[bka], cos_t[:, :], R_cos, ALU.mult)
            tt("dve", t2[:, :], R_t2, ps[bkb][:, :], R_ps[bkb], sin_t[:, :], R_sin, ALU.mult)
            tt("dve", krope[:, c0:c0 + T], R_krope, t1[:, :], R_t1, t2[:, :], R_t2, ALU.add)
            mark(f"t{j}:dq")
            slot = wnext(("w", "dq"))
            wv = wring[:, slot, 0:8 * QL].rearrange("p (k m) -> p k m", k=8)
            qb = [next_mm_bank() for _ in range(3)]
            for k in range(8):
                for cc in range(3):
                    mm(ps[qb[cc]][:, :], R_ps[qb[cc]], wv[:, k, cc * 128:(cc + 1) * 128], R_w[slot],
                       mid[:, k, :], R_mid[k], k == 0, k == 7)
            for cc in range(3):
                ts("dve", tA[:, 2 + cc, :], R_tA[2 + cc], ps[qb[cc]][:, :], R_ps[qb[cc]], 1.0, None, ALU.mult)
                act(sq[:, 2 + cc, :], R_sq[2 + cc], ps[qb[cc]][:, :], R_ps[qb[cc]], AF.Square)
            stats([(sq[:, cc, :], R_sq[cc]) for cc in range(2)], 5)
            rstd_from(5, KVL, RMS_EPS, out=rstd2, r_out=R_rstd2, scr=rscr2, r_scr=R_rscr2)
            for cc in range(2):
                stt(ckv[:, cc, :], R_ckv[cc], tA[:, cc, :], R_tA[cc], V("kv_norm_g", cc), rstd2[:, :], R_rstd2,
                    ALU.mult, ALU.mult)
            stats([(sq[:, 2 + cc, :], R_sq[2 + cc]) for cc in range(3)], 4)
            rstd_from(4, QL, RMS_EPS)
            for cc in range(3):
                stt(cq[:, cc, :], R_cq[cc], tA[:, 2 + cc, :], R_tA[2 + cc], V("q_norm_g", cc), rstd[:, :], R_rstd,
                    ALU.mult, ALU.mult)
            if j == 0:
                dump("ckv", ckv[:, :, :], R_ckv)
            mark(f"t{j}:kvb")
            slot = wnext(("w", "kvb"))
            wv = wring[:, slot, :].rearrange("p (k m) -> p k m", k=2)
            for hh in range(H):
                bank = next_mm_bank()
                for k in range(2):
                    mm(ps[bank][:, :], R_ps[bank], wv[:, k, hh * 128:(hh + 1) * 128], R_w[slot],
                       ckv[:, k, :], R_ckv[k], k == 0, k == 1)
                act(mid[:, 16 + hh, :], R_mid[16 + hh], ps[bank][:, :], R_ps[bank], AF.Identity)
            dump("ck_kn", None, None)
            for tb in range(4):
                for half in range(2):
                    bank = next_mm_bank()
                    for k in range(2):
                        mm(ps[bank][:, :], R_ps[bank], ckv[:, k, tb * 128:(tb + 1) * 128], R_ckv[k],
                           wv[:, k, 1024 + half * 512:1024 + (half + 1) * 512], R_w[slot], k == 0, k == 1)
                    dst = mid[:, 24 + 4 * half:24 + 4 * half + 4, tb * 128:(tb + 1) * 128]
                    src = ps[bank][:, :].rearrange("p (h d) -> p h d", h=4)
                    rw = R_mid[24 + 4 * half:24 + 4 * half + 4]
                    if half == 0:
                        tk.emit("dve", lambda: nc.vector.tensor_copy(dst, src), reads=[R_ps[bank]], writes=rw)
                    else:
                        tk.emit("act", lambda: nc.scalar.copy(dst, src), reads=[R_ps[bank]], writes=rw)
            dump("ck_v", None, None)
            dma("sp", Kc[:, :, c0:c0 + T].rearrange("h p t -> p h t"), R_Kcs[j], mid[:, 16:24, :], R_mid[16:24], ds_ks)
            dma("sp", Vc[:, :, c0:c0 + T].rearrange("h p t -> p h t"), R_Vcs[j], mid[:, 24:32, :], R_mid[24:32], ds_vs)

            dump("ck_kvst", None, None)
            dump("ck_dq", None, None)
            mark(f"t{j}:uq")
            slot = wnext(("w", "uq_0"))
            wv = wring[:, slot, 0:3 * 1024].rearrange("p (k m) -> p k m", k=3)
            for hh in range(H):
                bank = next_mm_bank()
                for k in range(3):
                    mm(ps[bank][:, :], R_ps[bank], wv[:, k, hh * 128:(hh + 1) * 128], R_w[slot],
                       cq[:, k, :], R_cq[k], k == 0, k == 2)
                act(mid[:, hh, :], R_mid[hh], ps[bank][:, :], R_ps[bank], AF.Identity)
            dump("ck_uq0", None, None)
            slot = wnext(("w", "uq_1"))
            wv = wring[:, slot, 0:3 * 1024].rearrange("p (k m) -> p k m", k=3)
            for p in range(4):
                ba = next_mm_bank()
                for k in range(3):
                    mm(ps[ba][:, :], R_ps[ba], wv[:, k, p * 128:(p + 1) * 128], R_w[slot],
                       cq[:, k, :], R_cq[k], k == 0, k == 2)
                bb = next_mm_bank()
                for k in range(3):
                    mm(ps[bb][:, :], R_ps[bb], wv[:, k, (4 + p) * 128:(5 + p) * 128], R_w[slot],
                       cq[:, k, :], R_cq[k], k == 0, k == 2)
                tt("dve", t1[:, :], R_t1, ps[ba][:, :], R_ps[ba], cos_t[:, :], R_cos, ALU.mult)
                tt("dve", t2[:, :], R_t2, ps[bb][:, :], R_ps[bb], sin_t[:, :], R_sin, ALU.mult)
                if j == 0:
                    tk.emit("dve", lambda p=p: nc.vector.memset(mid[64:128, 8 + p, :], 0.0), writes=[R_mid[8 + p]])
                    tk.emit("dve", lambda p=p: nc.vector.memset(mid[0:64, 12 + p, :], 0.0), writes=[R_mid[12 + p]])
                else:
                    tk.emit("pool", lambda p=p: nc.gpsimd.memset(mid[64:128, 8 + p, :], 0.0), writes=[R_mid[8 + p]])
                    tk.emit("pool", lambda p=p: nc.gpsimd.memset(mid[0:64, 12 + p, :], 0.0), writes=[R_mid[12 + p]])
                tt("dve", mid[0:64, 8 + p, :], R_mid[8 + p], t1[0:64, :], R_t1, t2[0:64, :], R_t2, ALU.add)
                tt("dve", mid[64:128, 12 + p, :], R_mid[12 + p], t1[64:128, :], R_t1, t2[64:128, :], R_t2, ALU.add)
            if j == 0:
                dump("qn", mid[:, 0:8, :], R_mid[0:8])
                dump("qr", mid[:, 8:12, :], R_mid[8:12])
                dump("krope", krope[:, 0:T], R_krope)

            nkc = 4 * (j + 1)
            n = T * (j + 1)
            LA = 2

            def kv_load(hh, lo=0, hi=None):
                hi = n if hi is None else hi
                sl = hh % 2
                blks = range(lo // T, (hi + T - 1) // T)
                dma("sp", kring[:, sl, lo:hi], R_kr[sl], Kc[hh, :, lo:hi], [R_Kcs[b_] for b_ in blks], ds_kl[sl])
                dma("sp", vring[:, sl, lo:hi], R_vr[sl], Vc[hh, :, lo:hi], [R_Vcs[b_] for b_ in blks], ds_vl[sl])

            seq = [(hh, kc) for hh in range(H) for kc in range(nkc)]
            pts = {}
            grp = {}
            den_started = {}
            den_q = []
            gcount = [0]
            acc32 = [t1, t2]
            R_acc32 = [R_t1, R_t2]

            def emit_S(i):
                hh, kc = seq[i]
                sl = hh % 2
                half = (hh // 4) * 64
                p = hh % 4
                c = kc - 4 * j
                q0 = 128 * c if c > 0 else 0
                sbk = next_mm_bank()
                mm(ps[sbk][:, q0:T], R_ps[sbk], kring[:, sl, kc * 128:(kc + 1) * 128], R_kr[sl],
                   mid[:, hh, q0:T], R_mid[hh], True, False)
                mm(ps[sbk][:, q0:T], R_ps[sbk], krope[:, kc * 128:(kc + 1) * 128], R_krope,
                   mid[:, 8 + hh, q0:T], R_mid[8 + hh], False, True)
                pi = 24 + (i % 4)
                pts[i] = (pi, q0)
                act(mid[:, pi, q0:T], R_mid[pi], ps[sbk][:, q0:T], R_ps[sbk], AF.Exp, scale=SCALE)
                if c >= 0:
                    if j == 0:
                        act(mid[64:128, pi, q0:q0 + 64], R_mid[pi], ps[sbk][64:128, q0:q0 + 64], R_ps[sbk],
                            AF.Copy, scale=0.0)
                    else:
                        tk.emit("pool", lambda pi=pi, q0=q0: nc.gpsimd.memset(mid[64:128, pi, q0:q0 + 64], 0.0),
                                writes=[R_mid[pi]])

            def emit_PV(i):
                hh, kc = seq[i]
                sl = hh % 2
                ob = 6 + (hh % 2)
                db = 4 + (hh % 2)
                pi, q0 = pts.pop(i)
                first = kc == 0
                last = kc == nkc - 1
                mm(ps[ob][:, q0:T], R_ps[ob], vring[:, sl, kc * 128:(kc + 1) * 128], R_vr[sl],
                   mid[:, pi, q0:T], R_mid[pi], first, last)

                def den_mm(rhs, r_rhs, q0_, last_, hh=hh, db=db):
                    st_ = not den_started.get(hh, False)
                    den_started[hh] = True
                    mm(ps[db][:, q0_:T], R_ps[db], ones[:, :], R_const, rhs, r_rhs, st_, last_)

                c = kc - 4 * j
                if c < 0:
                    pos = kc % 4
                    stg = grp.setdefault(hh, {})
                    if pos == 0:
                        stg["p0"] = pi
                    elif pos == 1:
                        g = gcount[0] % 2
                        stg["g"] = g
                        p0 = stg["p0"]
                        tt("dve", acc32[g][:, :], R_acc32[g], mid[:, p0, :], R_mid[p0], mid[:, pi, :], R_mid[pi], ALU.add)
                    elif pos == 2:
                        g = stg["g"]
                        tt("dve", acc32[g][:, :], R_acc32[g], acc32[g][:, :], R_acc32[g], mid[:, pi, :], R_mid[pi], ALU.add)
                    else:
                        g = stg["g"]
                        tt("dve", accb[:, g, :], R_accb[g], acc32[g][:, :], R_acc32[g], mid[:, pi, :], R_mid[pi], ALU.add)
                        gcount[0] += 1
                        den_q.append((i + 2, lambda g=g, den_mm=den_mm: den_mm(accb[:, g, :], R_accb[g], 0, False)))
                while den_q and (den_q[0][0] <= i or last):
                    den_q.pop(0)[1]()
                if c >= 0:
                    den_mm(mid[:, pi, q0:T], R_mid[pi], q0, last)
                if last:
                    act(rscr2[:, :], R_rscr2, ps[db][:, :], R_ps[db], AF.Ln)
                    act(rden[:, :], R_rden, rscr2[:, :], R_rscr2, AF.Exp, scale=-1.0)
                    tt("dve", mid[:, 16 + hh, :], R_mid[16 + hh], ps[ob][:, :], R_ps[ob], rden[:, :], R_rden, ALU.mult)
                    if hh + 2 < H:
                        kv_load(hh + 2)

            mark(f"t{j}:attn")
            kv_load(0, c0, n)
            kv_load(1, c0, n)
            for i in range(len(seq) + LA):
                if i < len(seq):
                    emit_S(i)
                if i - LA >= 0:
                    emit_PV(i - LA)
            if j == 0:
                dump("oT", mid[:, 16:24, :], R_mid[16:24])

            mark(f"t{j}:wo")
            for b in range(2):
                slot = wnext(("w", f"wo_{b}"))
                wv = wring[:, slot, :].rearrange("p (k m) -> p k m", k=8)
                for cc in range(4):
                    d = 4 * b + cc
                    bank = next_mm_bank()
                    for k in range(8):
                        mm(ps[bank][:, :], R_ps[bank], wv[:, k, cc * 128:(cc + 1) * 128], R_w[slot],
                           mid[:, 16 + k, :], R_mid[16 + k], k == 0, k == 7)
                    ts("dve", tA[:, d, :], R_tA[d], ps[bank][:, :], R_ps[bank], 1.0, None, ALU.mult)
                    act(sq[:, d, :], R_sq[d], ps[bank][:, :], R_ps[bank], AF.Square)
            mark(f"t{j}:post_mix1")
            postnorm("mix_post_g1", after=lambda c: hn_scale(c, 1))
            if j == 0:
                dump("h_mix1", h_t[:, :, :], R_h)
            mark(f"t{j}:ffn1")
            ffn(1, hook=(lambda: early_mixpre(j + 1)) if j + 1 < ntiles else None, pre_scaled=True)

            hb_, Rb_ = h_t, R_h
            stores.append(lambda hb_=hb_, Rb_=Rb_, c0=c0, j=j: dma(
                "sp", outT[:, c0:c0 + T].rearrange("(c p) t -> p c t", p=128), R_outs[j % 2],
                hb_[:, :, :], Rb_, ds_o[j % 2]))

        stopped = False
        try:
            for j in range(ntiles):
                tile_body(j)
        except _Stop:
            stopped = True
        run_pending()
        while stores:
            stores.pop(0)()
        assert stopped or st["next"] == len(sched), (st["next"], len(sched))
        toks = [R_outs[0].w, R_outs[1].w]
        for d_ in ds_dbg.values():
            if d_.val:
                toks.append((d_.key, d_.sem, d_.val, None))
        tk.wait_all("sp", toks)
        for en in ("pe", "act", "dve", "pool"):
            e = tk.engs[en]
            if e.count:
                tk.wait_all("sp", [(en, e.sem, e.count, en)])
        build_program.stats = (tk.ninst, tk.nwaits)
        build_program.marks = marks
    return nc


_CACHE = {}


def _prepare(inputs):
    inp = {k: np.asarray(v) for k, v in inputs.items()}
    wblocks, wnames = build_weight_blocks(inp)
    vecs = build_vecs(inp)
    cos2, sinS = rope_tables()
    return inp, wblocks, wnames, vecs, cos2, sinS


def kernel(**inputs):
    inp, wblocks, wnames, vecs, cos2, sinS = _prepare(inputs)
    nc = build_program(wblocks.shape[0], wnames)
    x = inp["x"]
    in_maps = []
    for b in range(NCORES):
        in_maps.append({
            "xT": np.ascontiguousarray(x[b].T),
            "w32": wblocks,
            "vecs": vecs,
            "cos2": cos2,
            "sinS": sinS,
        })
    res = run_bass_kernel_spmd(nc, in_maps, core_ids=list(range(NCORES)))
    out = np.stack([np.ascontiguousarray(np.asarray(r["outT"]).T) for r in res.results], 0)
    return out.astype(np.float32)
```
